# Optimizing a Trainium2 kernel written in Bass

```python
import jax, jax.numpy as jnp
from jax import lax
import numpy as np

D_MODEL = 1024
BATCH = 4
SEQ = 8192
DEPTH = 2

GRID_W = 64
WIN_ROWS = 8
WIN_COLS = 16
N_ATTN_HEADS = 8
HEAD_DIM = 64
D_ATTN = N_ATTN_HEADS * HEAD_DIM
N_FOURIER_GROUPS = 4
FOURIER_GROUP = 128
D_FOURIER = N_FOURIER_GROUPS * FOURIER_GROUP
D_MIX = D_ATTN + D_FOURIER
D_IN = 3 * D_ATTN + D_FOURIER
D_FF = 2816
ALPHA = (2.0 * DEPTH) ** 0.25
BETA = (8.0 * DEPTH) ** -0.25
LN_EPS = 1e-5
RMS_EPS = 1e-6
NEG_INF = -1e30

kernel_name = "hybrid_natten_fnet_macaron_deepnorm"


def layer_norm(x, g, b):
    xf = x.astype(jnp.float32)
    mu = jnp.mean(xf, axis=-1, keepdims=True)
    xc = xf - mu
    var = jnp.mean(xc * xc, axis=-1, keepdims=True)
    y = xc * lax.rsqrt(var + LN_EPS) * g.astype(jnp.float32) + b.astype(jnp.float32)
    return y.astype(x.dtype)


def rms_norm(x, g):
    xf = x.astype(jnp.float32)
    y = xf * lax.rsqrt(jnp.mean(xf * xf, axis=-1, keepdims=True) + RMS_EPS)
    return y * g.astype(jnp.float32)


def swiglu(x, w_gate, w_up, w_down):
    return (jax.nn.silu(x @ w_gate) * (x @ w_up)) @ w_down


def neighbourhood_attention(q, k, v, rpb):
    B, S, H, Dh = q.shape
    R = S // GRID_W
    KH = min(WIN_ROWS, R)

    def to_grid(t):
        return t.reshape(B, R, GRID_W, H, Dh).transpose(0, 3, 1, 2, 4).astype(jnp.float32)

    rows = jnp.arange(R)
    r0 = jnp.clip(rows - KH // 2, 0, R - KH)
    row_idx = r0[:, None] + jnp.arange(KH)[None, :]
    dr = row_idx - rows[:, None]

    cols = jnp.arange(GRID_W)
    c0 = jnp.clip(cols - WIN_COLS // 2, 0, GRID_W - WIN_COLS)
    dc = cols[None, :] - cols[:, None]
    in_win = (cols[None, :] >= c0[:, None]) & (cols[None, :] < c0[:, None] + WIN_COLS)
    dc_idx = jnp.clip(dc, -(WIN_COLS - 1), WIN_COLS - 1) + (WIN_COLS - 1)
    dr_idx = dr + (WIN_ROWS - 1)

    bias = rpb.astype(jnp.float32)[:, dr_idx[:, None, :, None], dc_idx[None, :, None, :]]
    mask = jnp.where(in_win, 0.0, NEG_INF).astype(jnp.float32)[None, None, :, None, :]
    bias = bias + mask

    qg = to_grid(q) * (Dh ** -0.5)
    kg = to_grid(k)[:, :, row_idx]
    vg = to_grid(v)[:, :, row_idx]

    scores = jnp.einsum('bhrqd,bhrnkd->bhrqnk', qg, kg) + bias[None]
    p = jax.nn.softmax(scores.reshape(B, H, R, GRID_W, KH * GRID_W), axis=-1)
    p = p.reshape(scores.shape)
    o = jnp.einsum('bhrqnk,bhrnkd->bhrqd', p, vg)
    return o.transpose(0, 2, 3, 1, 4).reshape(B, S, H * Dh)


def fourier_mix(u):
    B, S, _ = u.shape
    ug = u.reshape(B, S, N_FOURIER_GROUPS, FOURIER_GROUP).astype(jnp.float32)
    y = jnp.fft.fft2(ug, axes=(1, 3), norm="ortho").real
    return y.reshape(B, S, D_FOURIER)


def setup_inputs(seed: int = 0) -> dict:
    key = jax.random.key(seed)
    ks = jax.random.split(key, 24)
    f32 = jnp.float32

    def nrm(k, shape, scale):
        return jax.random.normal(k, shape, f32) * scale

    def gain(k, shape):
        return 1.0 + 0.05 * jax.random.normal(k, shape, f32)

    def bias(k, shape):
        return 0.02 * jax.random.normal(k, shape, f32)

    L, D, F = DEPTH, D_MODEL, D_FF
    x = jax.random.normal(ks[0], (BATCH, SEQ, D), f32)

    w_in = nrm(ks[7], (L, D, D_IN), D ** -0.5)
    col_scale = jnp.concatenate([jnp.ones((2 * D_ATTN,), f32), jnp.full((D_ATTN,), BETA, f32),
                                 jnp.ones((D_FOURIER,), f32)])
    w_in = w_in * col_scale

    return {
        "x": x,
        "ffn1_w_gate": nrm(ks[1], (L, D, F), D ** -0.5),
        "ffn1_w_up": nrm(ks[2], (L, D, F), D ** -0.5),
        "ffn1_w_down": nrm(ks[3], (L, F, D), BETA * F ** -0.5),
        "ln1_g": gain(ks[4], (L, D)),
        "ln1_b": bias(ks[5], (L, D)),
        "w_in": w_in,
        "rpb": nrm(ks[8], (L, N_ATTN_HEADS, 2 * WIN_ROWS - 1, 2 * WIN_COLS - 1), 0.1),
        "g_attn": gain(ks[9], (L, D_ATTN)),
        "g_fourier": gain(ks[10], (L, D_FOURIER)),
        "w_out": nrm(ks[11], (L, D_MIX, D), BETA * D_MIX ** -0.5),
        "ln2_g": gain(ks[12], (L, D)),
        "ln2_b": bias(ks[13], (L, D)),
        "ffn2_w_gate": nrm(ks[14], (L, D, F), D ** -0.5),
        "ffn2_w_up": nrm(ks[15], (L, D, F), D ** -0.5),
        "ffn2_w_down": nrm(ks[16], (L, F, D), BETA * F ** -0.5),
        "ln3_g": gain(ks[17], (L, D)),
        "ln3_b": bias(ks[18], (L, D)),
    }


def reference(x, ffn1_w_gate, ffn1_w_up, ffn1_w_down, ln1_g, ln1_b, w_in, rpb,
              g_attn, g_fourier, w_out, ln2_g, ln2_b, ffn2_w_gate, ffn2_w_up,
              ffn2_w_down, ln3_g, ln3_b):
    B, S, D = x.shape
    for l in range(DEPTH):
        x = layer_norm(ALPHA * x + 0.5 * swiglu(x, ffn1_w_gate[l], ffn1_w_up[l], ffn1_w_down[l]),
                       ln1_g[l], ln1_b[l])

        proj = x @ w_in[l]
        q = proj[..., :D_ATTN].reshape(B, S, N_ATTN_HEADS, HEAD_DIM)
        k = proj[..., D_ATTN:2 * D_ATTN].reshape(B, S, N_ATTN_HEADS, HEAD_DIM)
        v = proj[..., 2 * D_ATTN:3 * D_ATTN].reshape(B, S, N_ATTN_HEADS, HEAD_DIM)
        u = proj[..., 3 * D_ATTN:]

        attn = neighbourhood_attention(q, k, v, rpb[l])
        four = fourier_mix(u)
        merged = jnp.concatenate([rms_norm(attn, g_attn[l]), rms_norm(four, g_fourier[l])],
                                 axis=-1).astype(x.dtype)
        x = layer_norm(ALPHA * x + merged @ w_out[l], ln2_g[l], ln2_b[l])

        x = layer_norm(ALPHA * x + 0.5 * swiglu(x, ffn2_w_gate[l], ffn2_w_up[l], ffn2_w_down[l]),
                       ln3_g[l], ln3_b[l])
    return x
```

```python
from contextlib import ExitStack
import math
import numpy as np
import ml_dtypes
import concourse.bass as bass
import concourse.mybir as mybir
from concourse.bass_utils import run_bass_kernel_spmd

F32 = mybir.dt.float32
BF16 = mybir.dt.bfloat16
AF = mybir.ActivationFunctionType
ALU = mybir.AluOpType

D = 1024
DFF = 2816
NFC = DFF // 128
DEPTH = 2
ALPHA = (2.0 * DEPTH) ** 0.25
LN_EPS = 1e-5
RMS_EPS = 1e-6
NEG = -1e30
NCORES = 8
TOK = 4096
NT = TOK // 128


class Buf:
    __slots__ = ("name", "writers", "readers")

    def __init__(self, name):
        self.name = name
        self.writers = []
        self.readers = []


class Op:
    __slots__ = ("eng", "fn", "deps", "is_dma", "sem", "count", "milestone", "out_dram")

    def __init__(self, eng, fn):
        self.eng = eng
        self.fn = fn
        self.deps = []
        self.is_dma = False
        self.sem = None
        self.count = 0
        self.milestone = False
        self.out_dram = False


ENGINES = ("pe", "act", "dve", "pool", "sp")


class Sched:
    def __init__(self, nc, stack):
        self.nc = nc
        self.stack = stack
        self.streams = {e: [] for e in ENGINES}
        self.dma_sem_counts = {}
        self.dma_sems = {}
        self.eng_counts = {e: 0 for e in ENGINES}
        self.phase_sem_map = {}
        self.ncc = 0
        self.ncc = 0
        self.eng_sems = {e: stack.enter_context(nc.semaphore("sem_" + e)) for e in ENGINES if e != "sp"}
        self.bufs = {}

    def buf(self, name):
        if name not in self.bufs:
            self.bufs[name] = Buf(name)
        return self.bufs[name]

    def _track(self, op, reads, writes):
        deps = []
        for b in reads:
            deps.extend(b.writers)
            b.readers.append(op)
        for b in writes:
            if b.readers:
                deps.extend(b.readers)
                b.readers = []
                b.writers = [op]
            else:
                b.writers.append(op)
        seen = set()
        for d in deps:
            if d is op or id(d) in seen:
                continue
            seen.add(id(d))
            op.deps.append(d)

    def op(self, eng, fn, reads=(), writes=()):
        o = Op(eng, fn)
        self._track(o, reads, writes)
        self.streams[eng].append(o)
        return o

    def dma(self, queue, fn, sem=None, reads=(), writes=(), out_dram=False):
        key = reads[0].name + "!st" if (reads and (out_dram or not writes)) else writes[0].name
        if key not in self.phase_sem_map:
            self.phase_sem_map[key] = "d%d" % len(self.phase_sem_map)
        sem = self.phase_sem_map[key]
        o = Op(queue, fn)
        o.is_dma = True
        o.sem = sem
        self.dma_sem_counts[sem] = self.dma_sem_counts.get(sem, 0) + 16
        o.count = self.dma_sem_counts[sem]
        o.out_dram = out_dram
        self._track(o, reads, writes)
        self.streams[queue].append(o)
        return o

    def cc(self, fn, reads=(), writes=()):
        o = Op("pool", fn)
        o.is_dma = True
        o.sem = "cc%d" % self.ncc
        self.ncc += 1
        self.dma_sem_counts[o.sem] = 1
        o.count = 1
        o.out_dram = True
        self._track(o, reads, writes)
        self.streams["pool"].append(o)
        return o

    def emit(self):
        nc = self.nc
        for e in ENGINES:
            for o in self.streams[e]:
                for d in o.deps:
                    if not d.is_dma:
                        if d.eng == "pe" and o.eng == "pe" and not o.is_dma:
                            continue
                        d.milestone = True
        for e in ENGINES:
            for o in self.streams[e]:
                if o.milestone and not o.is_dma:
                    self.eng_counts[e] += 1
                    o.count = self.eng_counts[e]
        for k in self.dma_sem_counts:
            if k not in self.dma_sems:
                self.dma_sems[k] = self.stack.enter_context(nc.semaphore("dsem_" + k))
        eng_sem, dma_sem = self.eng_sems, self.dma_sems
        final = {}
        for e in ENGINES:
            for o in self.streams[e]:
                if o.is_dma and o.out_dram:
                    final[o.sem] = max(final.get(o.sem, 0), o.count)
        streams = self.streams

        def run(ename, eng):
            waited = {}
            for o in streams[ename]:
                need = {}
                for d in o.deps:
                    if d.is_dma:
                        key = ("d", d.sem)
                        s = dma_sem[d.sem]
                    else:
                        if d.eng == "pe" and ename == "pe" and not o.is_dma:
                            continue
                        key = ("e", d.eng)
                        s = eng_sem[d.eng]
                    if d.count > need.get(key, (None, 0))[1]:
                        need[key] = (s, d.count)
                for key, (s, c) in need.items():
                    if waited.get(key, 0) >= c:
                        continue
                    eng.wait_ge(s, c)
                    waited[key] = c
                inst = o.fn(eng)
                if o.is_dma:
                    if o.sem.startswith("cc"):
                        inst.then_inc(dma_sem[o.sem])
                    else:
                        inst.then_inc(dma_sem[o.sem], 16)
                elif o.milestone:
                    inst.then_inc(eng_sem[ename], 1)
            if ename == "sp":
                for k, c in final.items():
                    eng.wait_ge(dma_sem[k], c)

        with nc.Block() as block:
            @block.tensor
            def _(eng):
                run("pe", eng)

            @block.scalar
            def _(eng):
                run("act", eng)

            @block.vector
            def _(eng):
                run("dve", eng)

            @block.gpsimd
            def _(eng):
                run("pool", eng)

            @block.sync
            def _(eng):
                run("sp", eng)

        self.streams = {e: [] for e in ENGINES}
        self.bufs = {}
        self.phase_sem_map = {}


def build_F(nc, S, stack, pfx, x_in, x_out, xT_out, wg_d, wu_d, wd_d, lng_d, lnb_d, ident_d, ps, ntiles=NT):
    sb = lambda name, shape, dt: stack.enter_context(nc.sbuf_tensor(pfx + name, shape, dt))
    nblk = ntiles // 2
    wg = sb("wg", [128, 8, DFF], BF16)
    wu = sb("wu", [128, 8, DFF], BF16)
    wd = sb("wd", [128, NFC, D], BF16)
    lng = sb("lng", [128, D], F32)
    lnb = sb("lnb", [128, D], F32)
    ident = sb("ident", [128, 128], F32)
    xin = sb("xin", [128, 2, 2, D], F32)
    xT = sb("xT", [128, 2, 8, 256], BF16)
    sg = sb("sg", [128, 2, 256], F32)
    actT = sb("actT", [128, 2, 256], BF16)
    st6 = sb("st6", [128, 2, 12], F32)
    mv = sb("mv", [128, 2, 2], F32)
    ve = sb("ve", [128, 2, 1], F32)
    rstd = sb("rstd", [128, 2, 1], F32)
    mhalf = sb("mhalf", [128, 1], F32)
    xoT = sb("xoT", [128, 8, 256], BF16) if xT_out is not None else None

    B = lambda n: S.buf(pfx + n)
    wg_v = wg_d.rearrange("(c p) f -> p c f", p=128)
    wu_v = wu_d.rearrange("(c p) f -> p c f", p=128)
    wd_v = wd_d.rearrange("(c p) f -> p c f", p=128)
    GB = [0, 6, 12, 17, NFC]
    grp = lambda fc: 0 if fc < 6 else (1 if fc < 12 else (2 if fc < 17 else 3))
    for g in range(4):
        lo, hi = GB[g], GB[g + 1]
        S.dma("pool", lambda e, lo=lo, hi=hi: e.dma_start(out=wg[:, :, lo * 128:hi * 128], in_=wg_v[:, :, lo * 128:hi * 128]),
              None, writes=[B("wg%d" % g)])
        S.dma("pool", lambda e, lo=lo, hi=hi: e.dma_start(out=wu[:, :, lo * 128:hi * 128], in_=wu_v[:, :, lo * 128:hi * 128]),
              None, writes=[B("wu%d" % g)])
        S.dma("pool", lambda e, lo=lo, hi=hi: e.dma_start(out=wd[:, lo:hi, :], in_=wd_v[:, lo:hi, :]),
              None, writes=[B("wd%d" % g)])
    S.dma("sp", lambda e: e.dma_start(out=lng[:], in_=lng_d.partition_broadcast(128)), pfx + "c", writes=[B("lng")])
    S.dma("sp", lambda e: e.dma_start(out=lnb[:], in_=lnb_d.partition_broadcast(128)), pfx + "c", writes=[B("lnb")])
    S.dma("sp", lambda e: e.dma_start(out=ident[:], in_=ident_d), pfx + "c", writes=[B("ident")])
    S.op("pool", lambda e: e.memset(mhalf[:], -0.5), writes=[B("mhalf")])

    xin_v = x_in.rearrange("(b t p) d -> b p t d", t=2, p=128)
    xout_v = x_out.rearrange("(b t p) d -> b p t d", t=2, p=128)
    xTo_v = xT_out.rearrange("(c p) t -> p c t", p=128) if xT_out is not None else None

    acc = lambda j, hd: ps[:, 2 * j + hd, :]
    TP = ps[:, 6:8, :].rearrange("p b (c t) -> p (b c) t", t=128)

    def load(b):
        sl = b % 2
        S.dma("sp", lambda e: e.dma_start(out=xin[:, sl, :, :], in_=xin_v[b]), pfx + "xin%d" % sl,
              writes=[B("xin%d_0" % sl), B("xin%d_1" % sl)])

    def tr_in(b, j):
        sl = b % 2
        for dc in range(8):
            S.op("pe", lambda e, dc=dc: e.transpose(out=TP[:, dc, :], in_=xin[:, sl, j, dc * 128:(dc + 1) * 128], identity=ident[:]),
                 reads=[B("xin%d_%d" % (sl, j)), B("ident")], writes=[B("TP")])
        S.op("act", lambda e: e.copy(out=xT[:, sl, :, j * 128:(j + 1) * 128], in_=TP),
             reads=[B("TP")], writes=[B("xT%d" % sl)])

    def up(b, fc):
        sl = b % 2
        par = fc % 2
        gb = ps[:, 4 + par, 0:256]
        ub = ps[:, 4 + par, 256:512]
        for dc in range(8):
            S.op("pe", lambda e, dc=dc: e.matmul(gb, lhsT=wg[:, dc, fc * 128:(fc + 1) * 128], rhs=xT[:, sl, dc, :],
                                                 start=(dc == 0), stop=(dc == 7)),
                 reads=[B("wg%d" % grp(fc)), B("xT%d" % sl)], writes=[B("gu%d" % par)])
        for dc in range(8):
            S.op("pe", lambda e, dc=dc: e.matmul(ub, lhsT=wu[:, dc, fc * 128:(fc + 1) * 128], rhs=xT[:, sl, dc, :],
                                                 start=(dc == 0), stop=(dc == 7)),
                 reads=[B("wu%d" % grp(fc)), B("xT%d" % sl)], writes=[B("gu%d" % par)])
        S.op("act", lambda e: e.activation(out=sg[:, par, :], in_=gb, func=AF.Silu),
             reads=[B("gu%d" % par)], writes=[B("sg%d" % par)])
        S.op("dve", lambda e: e.scalar_tensor_tensor(out=actT[:, par, :], in0=sg[:, par, :], scalar=0.5, in1=ub,
                                                     op0=ALU.mult, op1=ALU.mult),
             reads=[B("sg%d" % par), B("gu%d" % par)], writes=[B("actT%d" % par)])

    def down(b, fc):
        par = fc % 2
        for j in range(2):
            for hd in range(2):
                S.op("pe", lambda e, j=j, hd=hd: e.matmul(acc(j, hd), lhsT=actT[:, par, j * 128:(j + 1) * 128],
                                                          rhs=wd[:, fc, hd * 512:(hd + 1) * 512],
                                                          start=(fc == 0), stop=(fc == NFC - 1)),
                     reads=[B("actT%d" % par), B("wd%d" % grp(fc))], writes=[B("acc%d" % j)])

    def epi_steps(b, j):
        sl = b % 2
        xb = B("xin%d_%d" % (sl, j))
        xt = lambda lo, hi: xin[:, sl, j, lo:hi]

        def s_pre():
            for hd in range(2):
                S.op("dve", lambda e, hd=hd: e.scalar_tensor_tensor(out=xt(hd * 512, (hd + 1) * 512), in0=xt(hd * 512, (hd + 1) * 512),
                                                                    scalar=ALPHA, in1=acc(j, hd), op0=ALU.mult, op1=ALU.add),
                     reads=[B("acc%d" % j), xb], writes=[xb])

        def s_stats():
            for hd in range(2):
                S.op("dve", lambda e, hd=hd: e.bn_stats(out=st6[:, j, hd * 6:(hd + 1) * 6], in_=xt(hd * 512, (hd + 1) * 512)),
                     reads=[xb], writes=[B("st6_%d" % j)])
            S.op("dve", lambda e: e.bn_aggr(out=mv[:, j, :], in_=st6[:, j, :]), reads=[B("st6_%d" % j)], writes=[B("mv%d" % j)])

        def s_rstd():
            S.op("pool", lambda e: e.tensor_scalar(out=ve[:, j, :], in0=mv[:, j, 1:2], scalar1=LN_EPS, scalar2=None, op0=ALU.add),
                 reads=[B("mv%d" % j)], writes=[B("ve%d" % j)])
            S.op("pool", lambda e: e.tensor_tensor(out=rstd[:, j, :], in0=ve[:, j, :], in1=mhalf[:], op=ALU.pow),
                 reads=[B("ve%d" % j), B("mhalf")], writes=[B("rstd%d" % j)])

        def s_norm1():
            S.op("dve", lambda e: e.scalar_tensor_tensor(out=xt(0, D), in0=xt(0, D), scalar=mv[:, j, 0:1], in1=lng[:],
                                                         op0=ALU.subtract, op1=ALU.mult),
                 reads=[xb, B("mv%d" % j), B("lng")], writes=[xb])

        def s_norm2():
            S.op("dve", lambda e: e.scalar_tensor_tensor(out=xt(0, D), in0=xt(0, D), scalar=rstd[:, j, :], in1=lnb[:],
                                                         op0=ALU.mult, op1=ALU.add),
                 reads=[xb, B("rstd%d" % j), B("lnb")], writes=[xb])

        return [s_pre, s_stats, s_rstd, s_norm1, s_norm2]

    def store(b):
        sl = b % 2
        S.dma("sp", lambda e: e.dma_start(out=xout_v[b], in_=xin[:, sl, :, :]), pfx + "xo%d" % sl,
              reads=[B("xin%d_0" % sl), B("xin%d_1" % sl)], out_dram=True)

    def tr_out(b, j):
        sl = b % 2
        for dc in range(8):
            S.op("pe", lambda e, dc=dc: e.transpose(out=TP[:, dc, :], in_=xin[:, sl, j, dc * 128:(dc + 1) * 128], identity=ident[:]),
                 reads=[B("xin%d_%d" % (sl, j)), B("ident")], writes=[B("TP")])
        S.op("act", lambda e: e.copy(out=xoT[:, :, j * 128:(j + 1) * 128], in_=TP),
             reads=[B("TP")], writes=[B("xoT")])

    def store_T(b):
        S.dma("sp", lambda e: e.dma_start(out=xTo_v[:, :, b * 256:(b + 1) * 256], in_=xoT[:]), pfx + "xTo",
              reads=[B("xoT")], out_dram=True)

    load(0)
    if nblk > 1:
        load(1)
    tr_in(0, 0)
    tr_in(0, 1)
    pending = []
    for b in range(nblk):
        up(b, 0)
        for fc in range(NFC):
            if fc + 1 < NFC:
                up(b, fc + 1)
            down(b, fc)
            if pending:
                pending.pop(0)()
            if b > 0:
                if xT_out is not None:
                    if fc == 10:
                        tr_out(b - 1, 0)
                    if fc == 11:
                        tr_out(b - 1, 1)
                        store_T(b - 1)
                if fc == 12:
                    store(b - 1)
                    if b + 1 < nblk:
                        load(b + 1)
            if b + 1 < nblk:
                if fc == 17:
                    tr_in(b + 1, 0)
                if fc == 19:
                    tr_in(b + 1, 1)
        e0 = epi_steps(b, 0)
        e1 = epi_steps(b, 1)
        e0[0]()
        e1[0]()
        pending = [s for pair in zip(e0[1:], e1[1:]) for s in pair]
    for s in pending:
        s()
    if xT_out is not None:
        tr_out(nblk - 1, 0)
        tr_out(nblk - 1, 1)
        store_T(nblk - 1)
    store(nblk - 1)


def build_P(nc, S, stack, pfx, xT_d, win_d, cs_d, qT_d, kT_d, vaug_d, ab_d, ps, ntok=TOK, xch=None):
    sb = lambda name, shape, dt: stack.enter_context(nc.sbuf_tensor(pfx + name, shape, dt))
    B = lambda n: S.buf(pfx + n)
    nblk = ntok // 512
    win = sb("win", [128, 8, 2048], BF16)
    cs = sb("cs", [128, 256], BF16)
    xT = sb("xT", [128, 2, 8, 512], BF16)
    uT = sb("uT", [128, 4, 512], BF16)
    qkst = sb("qkst", [128, 2, 512], BF16)
    vst = sb("vst", [128, 2, 8, 65], BF16)
    abst = sb("abst", [128, 2, 1024], BF16)

    win_v = win_d.rearrange("(c p) f -> p c f", p=128)
    for hh in range(2):
        S.dma("pool", lambda e, hh=hh: e.dma_start(out=win[:, :, hh * 1024:(hh + 1) * 1024], in_=win_v[:, :, hh * 1024:(hh + 1) * 1024]),
              pfx + "w", writes=[B("win")])
    S.dma("sp", lambda e: e.dma_start(out=cs[:], in_=cs_d), pfx + "c", writes=[B("cs")])
    S.op("dve", lambda e: e.memset(vst[:], 1.0), writes=[B("vst0"), B("vst1")])
    xT_v = xT_d.rearrange("(c p) t -> p c t", p=128)
    vaug_v = vaug_d.rearrange("(n p) f -> n p f", p=128)
    ab_v = ab_d.rearrange("(n p) f -> n p f", p=128)

    bank_i = [0]

    def bank():
        i = bank_i[0] % 8
        bank_i[0] += 1
        return i

    cnt = {"qk": 0, "v": 0, "ab": 0}

    def load(b):
        sl = b % 2
        S.dma("sp", lambda e: e.dma_start(out=xT[:, sl, :, :], in_=xT_v[:, :, b * 512:(b + 1) * 512]), pfx + "x%d" % sl,
              writes=[B("xT%d" % sl)])

    load(0)
    for b in range(nblk):
        sl = b % 2
        if b + 1 < nblk:
            load(b + 1)
        for oc in range(8):
            bk = bank()
            for dc in range(8):
                S.op("pe", lambda e, dc=dc, oc=oc, bk=bk, sl=sl: e.matmul(ps[:, bk, :], lhsT=win[:, dc, oc * 128:(oc + 1) * 128],
                                                                          rhs=xT[:, sl, dc, :], start=(dc == 0), stop=(dc == 7)),
                     reads=[B("win"), B("xT%d" % sl)], writes=[B("bank%d" % bk)])
            qs = cnt["qk"] % 2
            cnt["qk"] += 1
            if oc < 4:
                S.op("act", lambda e, bk=bk, qs=qs: e.mul(out=qkst[:, qs, :], in_=ps[:, bk, :], mul=0.125),
                     reads=[B("bank%d" % bk)], writes=[B("qkst%d" % qs)])
                dst = qT_d[oc * 128:(oc + 1) * 128, b * 512:(b + 1) * 512]
            else:
                S.op("act", lambda e, bk=bk, qs=qs: e.copy(out=qkst[:, qs, :], in_=ps[:, bk, :]),
                     reads=[B("bank%d" % bk)], writes=[B("qkst%d" % qs)])
                dst = kT_d[(oc - 4) * 128:(oc - 3) * 128, b * 512:(b + 1) * 512]
            S.dma("sp", lambda e, dst=dst, qs=qs: e.dma_start(out=dst, in_=qkst[:, qs, :]), pfx + "qk%d" % qs,
                  reads=[B("qkst%d" % qs)], writes=([B("kTd%d" % b)] if oc >= 4 else []), out_dram=True)
        for g in range(4):
            bk = bank()
            for dc in range(8):
                S.op("pe", lambda e, dc=dc, g=g, bk=bk, sl=sl: e.matmul(ps[:, bk, :], lhsT=win[:, dc, 1536 + g * 128:1536 + (g + 1) * 128],
                                                                        rhs=xT[:, sl, dc, :], start=(dc == 0), stop=(dc == 7)),
                     reads=[B("win"), B("xT%d" % sl)], writes=[B("bank%d" % bk)])
            S.op("dve", lambda e, bk=bk, g=g: e.tensor_copy(out=uT[:, g, :], in_=ps[:, bk, :]),
                 reads=[B("bank%d" % bk)], writes=[B("uT")])
        for j in range(4):
            tile_i = b * 4 + j
            bk = bank()
            for dc in range(8):
                S.op("pe", lambda e, dc=dc, bk=bk, sl=sl, j=j: e.matmul(ps[:, bk, :], lhsT=xT[:, sl, dc, j * 128:(j + 1) * 128],
                                                                        rhs=win[:, dc, 1024:1536], start=(dc == 0), stop=(dc == 7)),
                     reads=[B("win"), B("xT%d" % sl)], writes=[B("bank%d" % bk)])
            vs = cnt["v"] % 2
            cnt["v"] += 1
            S.op("dve", lambda e, bk=bk, vs=vs: e.tensor_copy(out=vst[:, vs, :, 0:64], in_=ps[:, bk, :].rearrange("p (h d) -> p h d", d=64)),
                 reads=[B("bank%d" % bk)], writes=[B("vst%d" % vs)])
            S.dma("sp", lambda e, vs=vs, tile_i=tile_i: e.dma_start(out=vaug_v[tile_i], in_=vst[:, vs, :, :].rearrange("p h d -> p (h d)")),
                  pfx + "v%d" % vs, reads=[B("vst%d" % vs)], writes=[B("vd%d" % b)], out_dram=True)
            bka = bank()
            bkb = bank()
            for g in range(4):
                S.op("pe", lambda e, g=g, bka=bka, j=j: e.matmul(ps[:, bka, g * 128:(g + 1) * 128], lhsT=uT[:, g, j * 128:(j + 1) * 128],
                                                                 rhs=cs[:, 0:128], start=True, stop=True),
                     reads=[B("uT"), B("cs")], writes=[B("bank%d" % bka)])
                S.op("pe", lambda e, g=g, bkb=bkb, j=j: e.matmul(ps[:, bkb, g * 128:(g + 1) * 128], lhsT=uT[:, g, j * 128:(j + 1) * 128],
                                                                 rhs=cs[:, 128:256], start=True, stop=True),
                     reads=[B("uT"), B("cs")], writes=[B("bank%d" % bkb)])
            asl = cnt["ab"] % 2
            cnt["ab"] += 1
            S.op("act", lambda e, bka=bka, asl=asl: e.copy(out=abst[:, asl, 0:512], in_=ps[:, bka, :]),
                 reads=[B("bank%d" % bka)], writes=[B("abst%d" % asl)])
            S.op("dve", lambda e, bkb=bkb, asl=asl: e.tensor_copy(out=abst[:, asl, 512:1024], in_=ps[:, bkb, :]),
                 reads=[B("bank%d" % bkb)], writes=[B("abst%d" % asl)])
            S.dma("sp", lambda e, asl=asl, tile_i=tile_i: e.dma_start(out=ab_v[tile_i], in_=abst[:, asl, :]), pfx + "ab%d" % asl,
                  reads=[B("abst%d" % asl)], writes=[B("abd%d" % b)], out_dram=True)
        if xch is not None:
            _exchange_block(S, B, b, nblk, kT_d, vaug_d, ab_d, xch)


def build_MA(nc, S, stack, pfx, ab_all_d, MA_d, T_d, ps, MA_pre=None):
    sb = lambda name, shape, dt: stack.enter_context(nc.sbuf_tensor(pfx + name, shape, dt))
    B = lambda n: S.buf(pfx + n)
    CH = 16
    ZA = sb("ZA", [128, 2, CH, 512], BF16)
    TA = sb("TA", [128, 2, CH, 512], BF16)
    if MA_pre is None:
        MA = sb("MA", [128, 128, 128], BF16)
        S.dma("sp", lambda e: e.dma_start(out=MA[:].rearrange("p s m -> p (s m)"), in_=MA_d), pfx + "c", writes=[B("MA")])
    else:
        MA = MA_pre
    abv = ab_all_d.rearrange("(s1 s2) (r c) -> r s1 s2 c", s2=128, r=2)

    def load(c):
        sl = c % 2
        for ri in range(2):
            S.dma("sp", lambda e, ri=ri: e.dma_start(out=ZA[ri * 64:(ri + 1) * 64, sl, :, :], in_=abv[ri][:, c * CH:(c + 1) * CH, :]),
                  pfx + "z%d" % sl, writes=[B("ZA%d" % sl)])

    load(0)
    cnt = [0]

    def chunk(c):
        sl = c % 2
        if c + 1 < 128 // CH:
            load(c + 1)
        for s2l in range(CH):
            s2 = c * CH + s2l
            n = cnt[0]
            cnt[0] += 1
            bk = n % 8
            S.op("pe", lambda e, s2=s2, s2l=s2l, bk=bk: e.matmul(ps[:, bk, :], lhsT=MA[:, s2, :], rhs=ZA[:, sl, s2l, :], start=True, stop=True),
                 reads=[B("MA"), B("ZA%d" % sl)], writes=[B("bank%d" % bk)])
            if n % 2 == 0:
                S.op("act", lambda e, s2l=s2l, bk=bk: e.copy(out=TA[:, sl, s2l, :], in_=ps[:, bk, :]),
                     reads=[B("bank%d" % bk)], writes=[B("TA%d" % sl)])
            else:
                S.op("dve", lambda e, s2l=s2l, bk=bk: e.tensor_copy(out=TA[:, sl, s2l, :], in_=ps[:, bk, :]),
                     reads=[B("bank%d" % bk)], writes=[B("TA%d" % sl)])
        S.dma("sp", lambda e: e.dma_start(out=T_d[:, c * CH:(c + 1) * CH, :], in_=TA[:, sl, :, :]), pfx + "t%d" % sl,
              reads=[B("TA%d" % sl)], out_dram=True)

    for c in range(128 // CH):
        chunk(c)


def build_MB(nc, S, stack, pfx, T_d, CB_d, F_d, rinvf, ps):
    sb = lambda name, shape, dt: stack.enter_context(nc.sbuf_tensor(pfx + name, shape, dt))
    B = lambda n: S.buf(pfx + n)
    CB = sb("CB", [128, 2, 64], BF16)
    ones = sb("ones", [128, 1], F32)
    TB = sb("TB", [128, 2, 2, 8, 512], BF16)
    fourT = sb("fourT", [128, 4, TOK], BF16)
    sq = sb("sq", [128, 2, 512], F32)
    ssq = sb("ssq", [1, TOK], F32)
    ve = sb("ve", [128, NT], F32)
    mh = sb("mh", [128, NT], F32)
    S.dma("sp", lambda e: e.dma_start(out=CB[:].rearrange("p a b -> p (a b)"), in_=CB_d), pfx + "c", writes=[B("CB")])
    S.op("dve", lambda e: e.memset(ones[:], 1.0), writes=[B("ones")])
    S.op("pool", lambda e: e.memset(mh[:], -0.5), writes=[B("mh")])
    Tv = T_d.rearrange("(r k) s c -> s r k c", r=2)
    f4 = fourT[:].rearrange("p c (a b) -> p c a b", b=64)
    ssq3 = ssq[:].rearrange("p (a b) -> p a b", b=64)

    def load(kc):
        sl = kc % 2
        for ri in range(2):
            S.dma("sp", lambda e, ri=ri: e.dma_start(out=TB[:, sl, ri, :, :], in_=Tv[:, ri, kc * 8:(kc + 1) * 8, :]), pfx + "tb%d" % sl,
                  writes=[B("TB%d" % sl)])

    load(0)
    cnt = [0]

    def kchunk(kc):
        sl = kc % 2
        if kc + 1 < 8:
            load(kc + 1)
        bs = 6 + kc % 2
        for cb in range(4):
            n = cnt[0]
            cnt[0] += 1
            bk = n % 6
            for k1l in range(8):
                for ri in range(2):
                    S.op("pe", lambda e, k1l=k1l, ri=ri, cb=cb, bk=bk: e.matmul(
                        ps[:, bk, k1l * 64:(k1l + 1) * 64], lhsT=TB[:, sl, ri, k1l, cb * 128:(cb + 1) * 128], rhs=CB[:, ri, :],
                        start=(ri == 0), stop=(ri == 1)),
                        reads=[B("TB%d" % sl), B("CB")], writes=[B("bank%d" % bk)])
            S.op("act", lambda e, cb=cb, bk=bk: e.copy(
                out=f4[:, cb, :, kc * 8:(kc + 1) * 8].rearrange("p a b -> p b a"),
                in_=ps[:, bk, :].rearrange("p (b a) -> p b a", a=64)),
                reads=[B("bank%d" % bk)], writes=[B("fourT")])
            sqs = n % 2
            S.op("act", lambda e, bk=bk, sqs=sqs: e.activation(out=sq[:, sqs, :], in_=ps[:, bk, :], func=AF.Square),
                 reads=[B("bank%d" % bk)], writes=[B("sq%d" % sqs)])
            S.op("pe", lambda e, cb=cb, bs=bs, sqs=sqs: e.matmul(ps[0:1, bs, :], lhsT=ones[:, 0:1], rhs=sq[:, sqs, :],
                                                                start=(cb == 0), stop=(cb == 3)),
                 reads=[B("sq%d" % sqs), B("ones")], writes=[B("bank%d" % bs)])
        S.op("dve", lambda e, bs=bs: e.tensor_copy(out=ssq3[:, :, kc * 8:(kc + 1) * 8].rearrange("p a b -> p b a"),
                                                   in_=ps[0:1, bs, :].rearrange("p (b a) -> p b a", a=64)),
             reads=[B("bank%d" % bs)], writes=[B("ssq")])

    for kc in range(8):
        kchunk(kc)
    for i in range(NT):
        S.op("pe", lambda e, i=i: e.matmul(ps[:, 0, i:i + 1], lhsT=ssq[0:1, i * 128:(i + 1) * 128], rhs=ones[0:1, 0:1],
                                           start=True, stop=True),
             reads=[B("ssq"), B("ones")], writes=[B("bank0")])
    S.op("dve", lambda e: e.tensor_scalar(out=ve[:], in0=ps[:, 0, 0:NT], scalar1=1.0 / 512, scalar2=RMS_EPS, op0=ALU.mult, op1=ALU.add),
         reads=[B("bank0")], writes=[B("ve")])
    S.op("pool", lambda e: e.tensor_tensor(out=rinvf[:], in0=ve[:], in1=mh[:], op=ALU.pow),
         reads=[B("ve"), B("mh")], writes=[S.buf("rinvf")])
    S.dma("sp", lambda e: e.dma_start(out=F_d.rearrange("c p t -> p c t"), in_=fourT[:]), pfx + "f",
          reads=[B("fourT")], out_dram=True)


def load_qk(S, qT, kT, qT_d, kT_d, kg_d, out_dram=False):
    kr = lambda ap: ap.rearrange("(c p) t -> p c t", p=128)
    S.dma("sp", lambda e: e.dma_start(out=qT[:], in_=kr(qT_d)), None, writes=[S.buf("qT_res")], out_dram=out_dram)
    if kg_d is None:
        S.dma("sp", lambda e: e.dma_start(out=kT[:], in_=kr(kT_d)), None, writes=[S.buf("kT_res")], out_dram=out_dram)
    else:
        S.dma("sp", lambda e: e.dma_start(out=kT[:, :, 0:256], in_=kr(kg_d[0:512, 256:512])), None, writes=[S.buf("kT_res")], out_dram=out_dram)
        S.dma("sp", lambda e: e.dma_start(out=kT[:, :, 256:256 + TOK], in_=kr(kT_d)), None, writes=[S.buf("kT_res")], out_dram=out_dram)
        S.dma("sp", lambda e: e.dma_start(out=kT[:, :, 256 + TOK:512 + TOK], in_=kr(kg_d[512:1024, 0:256])), None,
              writes=[S.buf("kT_res")], out_dram=out_dram)


def build_MC(nc, S, stack, pfx, x1_d, x2_d, qT_d, kTh_d, vh_d, F_d, rinvf, bint_d, bsp_d, wout_d, ga_d, gf_d,
             lng_d, lnb_d, ident_d, ps, dbg_tiles=None, dbg_stage=99, kg_d=None, vg_d=None, qk_pre=None):
    sb = lambda name, shape, dt: stack.enter_context(nc.sbuf_tensor(pfx + name, shape, dt))
    B = lambda n: S.buf(pfx + n)
    if qk_pre is None:
        qT = sb("qT", [128, 4, TOK], BF16)
        kT = sb("kT", [128, 4, 72 * 64], BF16)
    else:
        qT, kT = qk_pre
    va = sb("va", [128, 36, 520], BF16)
    bint = sb("bint", [128, 8, 5, 128], BF16)
    bsp = sb("bsp", [128, 8, 6, 128], BF16)
    wo = sb("wo", [128, 8, D], BF16)
    wst = sb("wst", [128, 2, D], F32)
    gcol = sb("gcol", [128, 8], F32)
    lng = sb("lng", [128, D], F32)
    lnb = sb("lnb", [128, D], F32)
    ident = sb("ident", [128, 128], F32)
    identb = sb("identb", [128, 128], BF16)
    xt = sb("xt", [128, 2, D], F32)
    ft = sb("ft", [128, 2, 4, 128], BF16)
    PT = sb("PT", [128, 2, 6, 128], BF16)
    osb = sb("osb", [128, 8, 64], F32)
    junk = sb("junk", [128, 512], F32)
    rden = sb("rden", [128, 8], F32)
    ssqa = sb("ssqa", [128, 1], F32)
    vea = sb("vea", [128, 1], F32)
    rinva = sb("rinva", [128, 1], F32)
    attnT = sb("attnT", [128, 4, 128], BF16)
    st6 = sb("st6", [128, 12], F32)
    mv = sb("mv", [128, 2], F32)
    ve = sb("ve", [128, 1], F32)
    rstd = sb("rstd", [128, 1], F32)
    mhalf = sb("mhalf", [128, 1], F32)

    if qk_pre is None:
        load_qk(S, qT, kT, qT_d, kTh_d, kg_d)
    if kg_d is None:
        S.dma("sp", lambda e: e.dma_start(out=va[:], in_=vh_d.rearrange("(t p) f -> p t f", p=128)), pfx + "c", writes=[B("va")])
    else:
        kr = lambda ap: ap.rearrange("(c p) t -> p c t", p=128)
        vr = lambda ap: ap.rearrange("(t p) f -> p t f", p=128)
        S.dma("sp", lambda e: e.dma_start(out=va[:, 0:2, :], in_=vr(vg_d[256:512, :])), pfx + "c", writes=[B("va")])
        S.dma("sp", lambda e: e.dma_start(out=va[:, 2:34, :], in_=vr(vh_d)), pfx + "c", writes=[B("va")])
        S.dma("sp", lambda e: e.dma_start(out=va[:, 34:36, :], in_=vr(vg_d[512:768, :])), pfx + "c", writes=[B("va")])
    bint_f = bint[:].rearrange("p h t q -> p (h t q)")
    for k in range(4):
        S.dma("pool", lambda e, k=k: e.dma_start(out=bint_f[:, k * 1280:(k + 1) * 1280], in_=bint_d[:, k * 1280:(k + 1) * 1280]),
              pfx + "cb", writes=[B("bint")])
    S.dma("sp", lambda e: e.dma_start(out=lng[:], in_=lng_d.partition_broadcast(128)), pfx + "c", writes=[B("lng")])
    S.dma("sp", lambda e: e.dma_start(out=lnb[:], in_=lnb_d.partition_broadcast(128)), pfx + "c", writes=[B("lnb")])
    S.dma("sp", lambda e: e.dma_start(out=ident[:], in_=ident_d), pfx + "c", writes=[B("ident")])
    S.dma("pool", lambda e: e.dma_start(out=identb[:], in_=ident_d), pfx + "cb", writes=[B("identb")])
    S.dma("sp", lambda e: e.dma_start(out=gcol[:, 0:4], in_=ga_d.rearrange("(c p) -> p c", p=128), allow_slow_non_contiguous=True), pfx + "c", writes=[B("gcol")])
    S.dma("sp", lambda e: e.dma_start(out=gcol[:, 4:8], in_=gf_d.rearrange("(c p) -> p c", p=128), allow_slow_non_contiguous=True), pfx + "c", writes=[B("gcol")])
    S.op("pool", lambda e: e.memset(mhalf[:], -0.5), writes=[B("mhalf")])
    for c in range(8):
        ws = c % 2
        S.dma("sp", lambda e, c=c, ws=ws: e.dma_start(out=wst[:, ws, :], in_=wout_d[c * 128:(c + 1) * 128, :]), pfx + "w%d" % ws,
              writes=[B("wst%d" % ws)])
        S.op("dve", lambda e, c=c, ws=ws: e.tensor_scalar(out=wo[:, c, :], in0=wst[:, ws, :], scalar1=gcol[:, c:c + 1], scalar2=None,
                                                          op0=ALU.mult),
             reads=[B("wst%d" % ws), B("gcol")], writes=[B("wo")])

    SPECIAL = {0: 0, 1: 1, NT - 2: 2, NT - 1: 3}
    x1v = x1_d.rearrange("(n p) d -> n p d", p=128)
    x2v = x2_d.rearrange("(n p) d -> n p d", p=128)
    Fv = F_d.rearrange("c p t -> p c t")
    Sflat = lambda sl: ps[:, 2 * sl:2 * sl + 2, :].rearrange("p b c -> p (b c)")
    O4 = ps[:, 4:6, 0:260].rearrange("p b (h d) -> p b h d", d=65)
    TR = ps[:, 4, :].rearrange("p (c t) -> p c t", t=128)

    def load_tile(i):
        sl = i % 2
        S.dma("sp", lambda e: e.dma_start(out=xt[:, sl, :], in_=x1v[i]), pfx + "x%d" % sl, writes=[B("xt%d" % sl)])
        S.dma("sp", lambda e: e.dma_start(out=ft[:, sl, :, :], in_=Fv[:, :, i * 128:(i + 1) * 128]), pfx + "x%d" % sl,
              writes=[B("ft%d" % sl)])

    def load_bsp(v):
        bsp_f = bsp[:].rearrange("p h t q -> p (h t q)")
        for k in range(4):
            S.dma("pool", lambda e, k=k: e.dma_start(out=bsp_f[:, k * 1536:(k + 1) * 1536], in_=bsp_d[v][:, k * 1536:(k + 1) * 1536]),
                  pfx + "bs", writes=[B("bsp")])

    TRb = ps[:, 7, :].rearrange("p (c t) -> p c t", t=128)
    ntl = dbg_tiles if dbg_tiles is not None else NT

    def geom(i):
        t0, nkt = i, 5
        if i == 0:
            nkt = 6
        if i == NT - 1:
            t0, nkt = NT - 2, 6
        if i in SPECIAL:
            return t0, nkt, bsp, "bsp"
        return t0, nkt, bint, "bint"

    def head_S(i, h, ss):
        t0, nkt, btab, bname = geom(i)
        hc, po = h // 2, (h % 2) * 64
        Sf = Sflat(ss)
        sbanks = [B("bank%d" % (2 * ss)), B("bank%d" % (2 * ss + 1))]
        S.op("pe", lambda e: e.matmul(Sf[:, 0:512], lhsT=identb[:], rhs=btab[:, h, 0:4, :].rearrange("p t q -> p (t q)"),
                                      start=True, stop=False),
             reads=[B("identb"), B(bname)], writes=[sbanks[0]])
        S.op("pe", lambda e: e.matmul(Sf[:, 512:nkt * 128], lhsT=identb[:], rhs=btab[:, h, 4:nkt, :].rearrange("p t q -> p (t q)"),
                                      start=True, stop=False),
             reads=[B("identb"), B(bname)], writes=[sbanks[1]])
        for tt in range(nkt):
            S.op("pe", lambda e, tt=tt: e.matmul(
                Sf[:, tt * 128:(tt + 1) * 128], lhsT=kT[po:po + 64, hc, (t0 + tt) * 128:(t0 + tt + 1) * 128],
                rhs=qT[po:po + 64, hc, i * 128:(i + 1) * 128], start=False, stop=(tt == 3 or tt == nkt - 1)),
                reads=[S.buf("kT_res"), S.buf("qT_res")], writes=[sbanks[tt // 4]])
        S.op("act", lambda e: e.activation(out=PT[:, ss, 0:nkt, :].rearrange("p t q -> p (t q)"), in_=Sf[:, 0:nkt * 128], func=AF.Exp),
             reads=sbanks, writes=[B("PT%d" % ss)])

    def head_PV(i, h, ss):
        t0, nkt, btab, bname = geom(i)
        for tt in range(nkt):
            S.op("pe", lambda e, tt=tt: e.matmul(
                O4[:, h // 4, h % 4, :], lhsT=PT[:, ss, tt, :], rhs=va[:, t0 + tt, h * 65:(h + 1) * 65],
                start=(tt == 0), stop=(tt == nkt - 1)),
                reads=[B("PT%d" % ss), B("va")], writes=[B("bank%d" % (4 + h // 4))])

    def tail_parts(i):
        sl = i % 2
        xb = B("xt%d" % sl)
        X = lambda lo, hi: xt[:, sl, lo:hi]
        osf = osb[:].rearrange("p h d -> p (h d)")
        obanks = [B("bank4"), B("bank5")]

        def t_norm_half(b):
            S.op("dve", lambda e: e.reciprocal(out=rden[:, 4 * b:4 * b + 4], in_=O4[:, b, :, 64]),
                 reads=[obanks[b]], writes=[B("rden")])
            S.op("dve", lambda e: e.tensor_tensor(out=osb[:, 4 * b:4 * b + 4, :], in0=O4[:, b, :, 0:64],
                                                  in1=rden[:, 4 * b:4 * b + 4].to_broadcast([128, 4, 64]), op=ALU.mult),
                 reads=[obanks[b], B("rden")], writes=[B("osb")])

        def t_normA():
            t_norm_half(0)

        def t_norm():
            t_norm_half(1)
            S.op("dve", lambda e: e.scalar_tensor_tensor(out=junk[:], in0=osf, scalar=1.0, in1=osf, op0=ALU.mult, op1=ALU.mult,
                                                         accum_out=ssqa[:, 0:1]),
                 reads=[B("osb")], writes=[B("junk"), B("ssqa")])
            S.op("dve", lambda e: e.tensor_scalar(out=vea[:], in0=ssqa[:], scalar1=1.0 / 512, scalar2=RMS_EPS, op0=ALU.mult, op1=ALU.add),
                 reads=[B("ssqa")], writes=[B("vea")])
            S.op("pool", lambda e: e.tensor_tensor(out=rinva[:], in0=vea[:], in1=mhalf[:], op=ALU.pow),
                 reads=[B("vea"), B("mhalf")], writes=[B("rinva")])
            S.op("act", lambda e: e.mul(out=X(0, D), in_=X(0, D), mul=ALPHA), reads=[xb], writes=[xb])

        def t_tr():
            for c in range(4):
                S.op("pe", lambda e, c=c: e.transpose(out=TRb[:, c, :], in_=osf[:, c * 128:(c + 1) * 128], identity=ident[:]),
                     reads=[B("osb"), B("ident")], writes=[B("bank7")])
            S.op("act", lambda e: e.copy(out=attnT[:], in_=TRb), reads=[B("bank7")], writes=[B("attnT")])

        def t_w(hd):
            def f():
                lo, hi = hd * 512, (hd + 1) * 512
                for c in range(4):
                    S.op("pe", lambda e, c=c: e.matmul(ps[:, 6, :], lhsT=attnT[:, c, :], rhs=wo[:, c, lo:hi], start=(c == 0), stop=(c == 3)),
                         reads=[B("attnT"), B("wo")], writes=[B("bank6")])
                for c in range(4):
                    S.op("pe", lambda e, c=c: e.matmul(ps[:, 7, :], lhsT=ft[:, sl, c, :], rhs=wo[:, 4 + c, lo:hi], start=(c == 0), stop=(c == 3)),
                         reads=[B("ft%d" % sl), B("wo")], writes=[B("bank7")])
                S.op("dve", lambda e: e.scalar_tensor_tensor(out=X(lo, hi), in0=ps[:, 6, :], scalar=rinva[:, 0:1], in1=X(lo, hi),
                                                             op0=ALU.mult, op1=ALU.add),
                     reads=[B("bank6"), B("rinva"), xb], writes=[xb])
                S.op("dve", lambda e: e.scalar_tensor_tensor(out=X(lo, hi), in0=ps[:, 7, :], scalar=rinvf[:, i:i + 1], in1=X(lo, hi),
                                                             op0=ALU.mult, op1=ALU.add),
                     reads=[B("bank7"), S.buf("rinvf"), xb], writes=[xb])
            return f

        def t_ln():
            for hd in range(2):
                S.op("dve", lambda e, hd=hd: e.bn_stats(out=st6[:, hd * 6:(hd + 1) * 6], in_=X(hd * 512, (hd + 1) * 512)),
                     reads=[xb], writes=[B("st6")])
            S.op("dve", lambda e: e.bn_aggr(out=mv[:], in_=st6[:]), reads=[B("st6")], writes=[B("mv")])
            S.op("pool", lambda e: e.tensor_scalar(out=ve[:], in0=mv[:, 1:2], scalar1=LN_EPS, scalar2=None, op0=ALU.add),
                 reads=[B("mv")], writes=[B("ve")])
            S.op("pool", lambda e: e.tensor_tensor(out=rstd[:], in0=ve[:], in1=mhalf[:], op=ALU.pow),
                 reads=[B("ve"), B("mhalf")], writes=[B("rstd")])
            S.op("dve", lambda e: e.scalar_tensor_tensor(out=X(0, D), in0=X(0, D), scalar=mv[:, 0:1], in1=lng[:],
                                                         op0=ALU.subtract, op1=ALU.mult),
                 reads=[xb, B("mv"), B("lng")], writes=[xb])
            S.op("dve", lambda e: e.scalar_tensor_tensor(out=X(0, D), in0=X(0, D), scalar=rstd[:, 0:1], in1=lnb[:],
                                                         op0=ALU.mult, op1=ALU.add),
                 reads=[xb, B("rstd"), B("lnb")], writes=[xb])
            S.dma("sp", lambda e: e.dma_start(out=x2v[i], in_=xt[:, sl, :]), None, reads=[xb], out_dram=True)

        return [t_norm, t_tr, t_w(0), t_w(1), t_ln, t_normA]

    load_tile(0)
    load_bsp(0)
    seq = [(i, h) for i in range(ntl) for h in range(8)]
    pending = {}
    parts = tail_parts(0)
    head_S(0, 0, 0)
    for g, (i, h) in enumerate(seq):
        if g + 1 < len(seq):
            ni, nh = seq[g + 1]
            if nh == 0:
                if ni == 1:
                    load_bsp(1)
                elif ni == 2:
                    load_bsp(2)
                elif ni == NT - 1:
                    load_bsp(3)
            head_S(ni, nh, (g + 1) % 2)
        head_PV(i, h, g % 2)
        if h in pending:
            pending.pop(h)()
        if h == 3:
            parts[5]()
        if h == 6 and i + 1 < ntl:
            load_tile(i + 1)
        if h == 7:
            parts[0]()
            pending = {1: parts[1], 2: parts[2], 4: parts[3], 5: parts[4]}
            if i + 1 < ntl:
                parts = tail_parts(i + 1)
    for h in sorted(pending):
        pending[h]()


def host_consts(gathered_layout=True):
    n = np.arange(128)
    ang = 2 * np.pi * np.outer(n, n) / 128.0
    cs = (np.concatenate([np.cos(ang), np.sin(ang)], 1) / np.sqrt(128.0)).astype(ml_dtypes.bfloat16)
    s1 = np.arange(64)[:, None, None]
    s2 = np.arange(128)[None, :, None]
    k1 = np.arange(64)[None, None, :]
    th = 2 * np.pi * (s1 * k1 / 64.0 + s2 * k1 / 8192.0)
    c, s = np.cos(th) / 8.0, np.sin(th) / 8.0
    MA = np.zeros((2, 64, 128, 2, 64), np.float64)
    MA[0, :, :, 0, :] = c
    MA[1, :, :, 0, :] = -s
    MA[0, :, :, 1, :] = s
    MA[1, :, :, 1, :] = c
    if gathered_layout:
        pi = np.arange(64)
        s1_of_pi = ((pi // 4) % 2) * 32 + (pi // 8) * 4 + pi % 4
        MA = MA[:, s1_of_pi]
    MA = MA.reshape(128, 128 * 128).astype(ml_dtypes.bfloat16)
    CB = []
    for p in range(2):
        k2 = (64 * p + np.arange(64))[None, :]
        a2 = 2 * np.pi * np.arange(128)[:, None] * k2 / 128.0
        cb = np.stack([np.cos(a2), -np.sin(a2)], 1) / (8.0 * np.sqrt(2.0))
        CB.append(cb.reshape(128, 128).astype(ml_dtypes.bfloat16))
    ident = np.eye(128, dtype=np.float32)
    return cs, MA, CB, ident


def _bias_tab(rpb_l, p, i, t0, nkt, pad):
    kp = np.arange(128)[:, None, None]
    tt = np.arange(nkt)[None, :, None]
    q = np.arange(128)[None, None, :]
    krow = 64 * p + 2 * (t0 + tt) + kp // 64 - 4
    kc = kp % 64
    qrow = 64 * p + 2 * i + q // 64
    qc = q % 64
    r0 = np.clip(qrow - 4, 0, 120)
    c0 = np.clip(qc - 8, 0, 48)
    valid = (krow >= 0) & (krow <= 127) & (krow >= r0) & (krow <= r0 + 7) & (kc >= c0) & (kc <= c0 + 15)
    dr = np.clip(krow - qrow + 7, 0, 14)
    dc = np.clip(kc - qc, -15, 15) + 15
    dr, dc, valid = np.broadcast_arrays(dr, dc, valid)
    out = np.full((128, 8, pad, 128), NEG, np.float32)
    for h in range(8):
        out[:, h, :nkt, :] = np.where(valid, rpb_l[h][dr, dc], np.float32(NEG))
    return out


def host_bias(rpb_l, p):
    bint = _bias_tab(rpb_l, p, 2, 2, 5, 5).reshape(128, 8 * 5 * 128)
    sp = [_bias_tab(rpb_l, p, 0, 0, 6, 6), _bias_tab(rpb_l, p, 1, 1, 5, 6),
          _bias_tab(rpb_l, p, NT - 2, NT - 2, 5, 6), _bias_tab(rpb_l, p, NT - 1, NT - 2, 6, 6)]
    bsp = np.stack([t.reshape(128, 8 * 6 * 128) for t in sp], 0)
    return bint, bsp


def host_halo(kT_pair, va_pair, p):
    kTh = np.zeros((512, 72 * 64), kT_pair.dtype)
    vh = np.zeros((72 * 64, 520), va_pair.dtype)
    g0 = 64 * p - 4
    lo, hi = max(g0, 0), min(g0 + 72, 128)
    kTh[:, (lo - g0) * 64:(hi - g0) * 64] = kT_pair[:, lo * 64:hi * 64]
    vh[(lo - g0) * 64:(hi - g0) * 64, :] = va_pair[lo * 64:hi * 64, :]
    return kTh, vh


def _dram(nc, name, shape, dtype, kind):
    return nc.dram_tensor(name, list(shape), dtype, kind=kind).ap()


def _decl_F(nc, sfx):
    I = lambda n, s: _dram(nc, n + sfx, s, F32, "ExternalInput")
    return dict(wg=I("wg", [D, DFF]), wu=I("wu", [D, DFF]), wd=I("wd", [DFF, D]), lng=I("lng", [1, D]), lnb=I("lnb", [1, D]))


def _run_F(nc, S, ps, pfx, x_in, x_out, xT_out, w, ident):
    with ExitStack() as st:
        build_F(nc, S, st, pfx, x_in, x_out, xT_out, w["wg"], w["wu"], w["wd"], w["lng"][0, :], w["lnb"][0, :], ident, ps)
        S.emit()


def build_launch_A(with_prev_F):
    nc = bass.Bass("TRN2", target_bir_lowering=False)
    x = _dram(nc, "x", [TOK, D], F32, "ExternalInput")
    ident = _dram(nc, "ident", [128, 128], F32, "ExternalInput")
    cs = _dram(nc, "cs", [128, 256], BF16, "ExternalInput")
    win = _dram(nc, "win", [D, 2048], F32, "ExternalInput")
    wprev = _decl_F(nc, "_p") if with_prev_F else None
    w1 = _decl_F(nc, "_1")
    x1 = _dram(nc, "x1", [TOK, D], F32, "ExternalOutput")
    qT = _dram(nc, "qT", [512, TOK], BF16, "ExternalOutput")
    kT = _dram(nc, "kT", [512, TOK], BF16, "ExternalOutput")
    vaug = _dram(nc, "vaug", [TOK, 520], BF16, "ExternalOutput")
    ab = _dram(nc, "ab", [TOK, 1024], BF16, "ExternalOutput")
    x1T = _dram(nc, "x1T", [D, TOK], BF16, "Internal")
    x3 = _dram(nc, "x3", [TOK, D], F32, "Internal") if with_prev_F else None
    with ExitStack() as stack:
        ps = stack.enter_context(nc.psum_tensor("ps", [128, 8, 512], F32))
        S = Sched(nc, stack)
        xin = x
        if with_prev_F:
            _run_F(nc, S, ps, "fp_", x, x3, None, wprev, ident)
            xin = x3
        _run_F(nc, S, ps, "f1_", xin, x1, x1T, w1, ident)
        with ExitStack() as st:
            build_P(nc, S, st, "p_", x1T, win, cs, qT, kT, vaug, ab, ps)
            S.emit()
    return nc


def build_launch_B():
    nc = bass.Bass("TRN2", target_bir_lowering=False)
    I = lambda n, s, d=F32: _dram(nc, n, s, d, "ExternalInput")
    x1 = I("x1", [TOK, D])
    qT = I("qT", [512, TOK], BF16)
    kTh = I("kTh", [512, 72 * 64], BF16)
    vh = I("vh", [72 * 64, 520], BF16)
    ab_all = I("ab_all", [8192, 1024], BF16)
    MA = I("MA", [128, 128 * 128], BF16)
    CB = I("CB", [128, 128], BF16)
    bint = I("bint", [128, 5120])
    bsp = I("bsp", [4, 128, 6144])
    wout = I("wout", [D, D])
    ga = I("ga", [512])
    gf = I("gf", [512])
    lng = I("lng", [1, D])
    lnb = I("lnb", [1, D])
    ident = I("ident", [128, 128])
    x2 = _dram(nc, "x2", [TOK, D], F32, "ExternalOutput")
    T_d = _dram(nc, "T_d", [128, 128, 512], BF16, "Internal")
    F_d = _dram(nc, "F_d", [4, 128, TOK], BF16, "Internal")
    with ExitStack() as stack:
        ps = stack.enter_context(nc.psum_tensor("ps", [128, 8, 512], F32))
        rinvf = stack.enter_context(nc.sbuf_tensor("rinvf", [128, NT], F32))
        S = Sched(nc, stack)
        with ExitStack() as st:
            build_MA(nc, S, st, "ma_", ab_all, MA, T_d, ps)
            S.emit()
        with ExitStack() as st:
            build_MB(nc, S, st, "mb_", T_d, CB, F_d, rinvf, ps)
            S.emit()
        with ExitStack() as st:
            build_MC(nc, S, st, "mc_", x1, x2, qT, kTh, vh, F_d, rinvf, bint, bsp, wout, ga, gf, lng[0, :], lnb[0, :], ident, ps)
            S.emit()
    return nc


def build_launch_D():
    nc = bass.Bass("TRN2", target_bir_lowering=False)
    x = _dram(nc, "x", [TOK, D], F32, "ExternalInput")
    ident = _dram(nc, "ident", [128, 128], F32, "ExternalInput")
    w = _decl_F(nc, "_1")
    out = _dram(nc, "out", [TOK, D], F32, "ExternalOutput")
    with ExitStack() as stack:
        ps = stack.enter_context(nc.psum_tensor("ps", [128, 8, 512], F32))
        S = Sched(nc, stack)
        _run_F(nc, S, ps, "f1_", x, out, None, w, ident)
    return nc


def _fw(inputs, which, l, sfx):
    c = np.ascontiguousarray
    return {"wg" + sfx: c(inputs[which + "_w_gate"][l]), "wu" + sfx: c(inputs[which + "_w_up"][l]),
            "wd" + sfx: c(inputs[which + "_w_down"][l])}


PAIRS = [[0, 1], [2, 3], [4, 5], [6, 7]]


def _exchange_block(S, B, b, nblk, kT_d, vaug_d, ab_d, xch):
    aball, pack, packg = xch["aball"], xch["pack"], xch["packg"]
    S.cc(lambda e: e.collective_compute("AllGather", ALU.bypass, replica_groups=PAIRS, ins=[ab_d[b * 512:(b + 1) * 512, :]],
                                        outs=[aball[b * 1024:(b + 1) * 1024, :]]),
         reads=[B("abd%d" % b)], writes=[B("aball")])
    if b == 0:
        S.dma("pool", lambda e: e.dma_start(out=pack[:, 0:256], in_=kT_d[:, 0:256]), None, reads=[B("kTd0")], writes=[B("pack")])
        S.dma("pool", lambda e: e.dma_start(out=pack[0:256, 512:1032], in_=vaug_d[0:256, :]), None, reads=[B("vd0")], writes=[B("pack")])
    if b == nblk - 1:
        S.dma("pool", lambda e: e.dma_start(out=pack[:, 256:512], in_=kT_d[:, TOK - 256:TOK]), None,
              reads=[B("kTd%d" % b)], writes=[B("pack")])
        S.dma("pool", lambda e: e.dma_start(out=pack[256:512, 512:1032], in_=vaug_d[TOK - 256:TOK, :]), None,
              reads=[B("vd%d" % b)], writes=[B("pack")])
        S.cc(lambda e: e.collective_compute("AllGather", ALU.bypass, replica_groups=PAIRS, ins=[pack], outs=[packg]),
             reads=[B("pack")], writes=[B("packg")])


def build_fused():
    nc = bass.Bass("TRN2", target_bir_lowering=False)
    I = lambda n, s, d=F32: _dram(nc, n, s, d, "ExternalInput")
    N = lambda n, s, d=F32: _dram(nc, n, s, d, "Internal")
    x = I("x", [TOK, D])
    out = _dram(nc, "out", [TOK, D], F32, "ExternalOutput")
    ident = I("ident", [128, 128])
    cs = I("cs", [128, 256], BF16)
    MA = I("MA", [128, 128 * 128], BF16)
    CB = I("CB", [128, 128], BF16)
    L = []
    for l in range(DEPTH):
        s = "_%d" % l
        L.append(dict(
            f1=_decl_F(nc, "_a%d" % l), f2=_decl_F(nc, "_b%d" % l),
            win=I("win" + s, [D, 2048]), wout=I("wout" + s, [D, D]), ga=I("ga" + s, [512]), gf=I("gf" + s, [512]),
            lng2=I("lng2" + s, [1, D]), lnb2=I("lnb2" + s, [1, D]), bint=I("bint" + s, [128, 5120]), bsp=I("bsp" + s, [4, 128, 6144]),
            x1=N("x1" + s, [TOK, D]), x1T=N("x1T" + s, [D, TOK], BF16), qT=N("qT" + s, [512, TOK], BF16),
            kT=N("kT" + s, [512, TOK], BF16), vaug=N("vaug" + s, [TOK, 520], BF16), ab=N("ab" + s, [TOK, 1024], BF16),
            aball=N("aball" + s, [2 * TOK, 1024], BF16), pack=N("pack" + s, [512, 1032], BF16), packg=N("packg" + s, [1024, 1032], BF16),
            T=N("T" + s, [128, 128, 512], BF16), F=N("F" + s, [4, 128, TOK], BF16),
            x2=N("x2" + s, [TOK, D]), x3=(N("x3" + s, [TOK, D]) if l + 1 < DEPTH else out)))
    with ExitStack() as stack:
        ps = stack.enter_context(nc.psum_tensor("ps", [128, 8, 512], F32))
        rinvf = stack.enter_context(nc.sbuf_tensor("rinvf", [128, NT], F32))
        S = Sched(nc, stack)
        xin = x
        for l in range(DEPTH):
            t = L[l]
            _run_F(nc, S, ps, "f1_%d_" % l, xin, t["x1"], t["x1T"], t["f1"], ident)
            with ExitStack() as so:
                MA_sb = so.enter_context(nc.sbuf_tensor("MA_sb%d" % l, [128, 128, 128], BF16))
                with ExitStack() as st:
                    S.dma("sp", lambda e: e.dma_start(out=MA_sb[:].rearrange("p s m -> p (s m)"), in_=MA), None,
                          writes=[S.buf("MA_pre")], out_dram=True)
                    build_P(nc, S, st, "p%d_" % l, t["x1T"], t["win"], cs, t["qT"], t["kT"], t["vaug"], t["ab"], ps, xch=t)
                    S.emit()
                with ExitStack() as st:
                    build_MA(nc, S, st, "ma%d_" % l, t["aball"], MA, t["T"], ps, MA_pre=MA_sb)
                    S.emit()
            with ExitStack() as so:
                qT_sb = so.enter_context(nc.sbuf_tensor("qT_sb%d" % l, [128, 4, TOK], BF16))
                kT_sb = so.enter_context(nc.sbuf_tensor("kT_sb%d" % l, [128, 4, 72 * 64], BF16))
                kg, vg = t["packg"][:, 0:512], t["packg"][:, 512:1032]
                with ExitStack() as st:
                    load_qk(S, qT_sb, kT_sb, t["qT"], t["kT"], kg, out_dram=True)
                    build_MB(nc, S, st, "mb%d_" % l, t["T"], CB, t["F"], rinvf, ps)
                    S.emit()
                with ExitStack() as st:
                    build_MC(nc, S, st, "mc%d_" % l, t["x1"], t["x2"], t["qT"], t["kT"], t["vaug"], t["F"], rinvf, t["bint"], t["bsp"],
                             t["wout"], t["ga"], t["gf"], t["lng2"][0, :], t["lnb2"][0, :], ident, ps, kg_d=kg, vg_d=vg,
                             qk_pre=(qT_sb, kT_sb))
                    S.emit()
            _run_F(nc, S, ps, "f2_%d_" % l, t["x2"], t["x3"], None, t["f2"], ident)
            xin = t["x3"]
    return nc


def kernel(**inputs):
    inputs = {k: np.asarray(v) for k, v in inputs.items()}
    cs, MAc, CBc, ident = host_consts()
    c = np.ascontiguousarray
    cores = list(range(NCORES))
    xs = inputs["x"].reshape(NCORES, TOK, D)
    shared = {"ident": ident, "cs": cs, "MA": MAc}
    for l in range(DEPTH):
        s = "_%d" % l
        for nm, which, sfx in (("ffn1", "a", "ln1"), ("ffn2", "b", "ln3")):
            sf = "_%s%d" % (which, l)
            shared["wg" + sf] = c(inputs[nm + "_w_gate"][l])
            shared["wu" + sf] = c(inputs[nm + "_w_up"][l])
            shared["wd" + sf] = c(inputs[nm + "_w_down"][l])
            shared["lng" + sf] = c(inputs[sfx + "_g"][l:l + 1])
            shared["lnb" + sf] = c(inputs[sfx + "_b"][l:l + 1])
        shared["win" + s] = c(inputs["w_in"][l])
        shared["wout" + s] = c(inputs["w_out"][l])
        shared["ga" + s] = c(inputs["g_attn"][l])
        shared["gf" + s] = c(inputs["g_fourier"][l])
        shared["lng2" + s] = c(inputs["ln2_g"][l:l + 1])
        shared["lnb2" + s] = c(inputs["ln2_b"][l:l + 1])
    ims = []
    for i in cores:
        p = i % 2
        m = dict(shared, x=c(xs[i]), CB=CBc[p])
        for l in range(DEPTH):
            bint, bsp = host_bias(inputs["rpb"][l], p)
            m["bint_%d" % l] = bint
            m["bsp_%d" % l] = bsp
        ims.append(m)
    nc = build_fused()
    res = run_bass_kernel_spmd(nc, ims, core_ids=cores).results
    out = np.stack([np.asarray(res[i]["out"]) for i in cores], 0).reshape(4, 8192, D)
    return out.astype(np.float32)
```

```python
from contextlib import ExitStack
import math
import numpy as np
import ml_dtypes
import concourse.bass as bass
import concourse.mybir as mybir
from concourse.bass_utils import run_bass_kernel_spmd

F32 = mybir.dt.float32
BF16 = mybir.dt.bfloat16
AF = mybir.ActivationFunctionType
ALU = mybir.AluOpType

D = 1024
DFF = 2816
NFC = DFF // 128
DEPTH = 2
ALPHA = (2.0 * DEPTH) ** 0.25
LN_EPS = 1e-5
RMS_EPS = 1e-6
NEG = -1e30
NCORES = 8
TOK = 4096
NT = TOK // 128


class Buf:
    __slots__ = ("name", "writers", "readers")

    def __init__(self, name):
        self.name = name
        self.writers = []
        self.readers = []


class Op:
    __slots__ = ("eng", "fn", "deps", "is_dma", "sem", "count", "milestone", "out_dram")

    def __init__(self, eng, fn):
        self.eng = eng
        self.fn = fn
        self.deps = []
        self.is_dma = False
        self.sem = None
        self.count = 0
        self.milestone = False
        self.out_dram = False


ENGINES = ("pe", "act", "dve", "pool", "sp")


class Sched:
    def __init__(self, nc, stack):
        self.nc = nc
        self.stack = stack
        self.streams = {e: [] for e in ENGINES}
        self.dma_sem_counts = {}
        self.dma_sems = {}
        self.eng_counts = {e: 0 for e in ENGINES}
        self.phase_sem_map = {}
        self.ncc = 0
        self.ncc = 0
        self.eng_sems = {e: stack.enter_context(nc.semaphore("sem_" + e)) for e in ENGINES if e != "sp"}
        self.bufs = {}

    def buf(self, name):
        if name not in self.bufs:
            self.bufs[name] = Buf(name)
        return self.bufs[name]

    def _track(self, op, reads, writes):
        deps = []
        for b in reads:
            deps.extend(b.writers)
            b.readers.append(op)
        for b in writes:
            if b.readers:
                deps.extend(b.readers)
                b.readers = []
                b.writers = [op]
            else:
                b.writers.append(op)
        seen = set()
        for d in deps:
            if d is op or id(d) in seen:
                continue
            seen.add(id(d))
            op.deps.append(d)

    def op(self, eng, fn, reads=(), writes=()):
        o = Op(eng, fn)
        self._track(o, reads, writes)
        self.streams[eng].append(o)
        return o

    def dma(self, queue, fn, sem=None, reads=(), writes=(), out_dram=False):
        key = reads[0].name + "!st" if (reads and (out_dram or not writes)) else writes[0].name
        if key not in self.phase_sem_map:
            self.phase_sem_map[key] = "d%d" % len(self.phase_sem_map)
        sem = self.phase_sem_map[key]
        o = Op(queue, fn)
        o.is_dma = True
        o.sem = sem
        self.dma_sem_counts[sem] = self.dma_sem_counts.get(sem, 0) + 16
        o.count = self.dma_sem_counts[sem]
        o.out_dram = out_dram
        self._track(o, reads, writes)
        self.streams[queue].append(o)
        return o

    def cc(self, fn, reads=(), writes=()):
        o = Op("pool", fn)
        o.is_dma = True
        o.sem = "cc%d" % self.ncc
        self.ncc += 1
        self.dma_sem_counts[o.sem] = 1
        o.count = 1
        o.out_dram = True
        self._track(o, reads, writes)
        self.streams["pool"].append(o)
        return o

    def emit(self):
        nc = self.nc
        for e in ENGINES:
            for o in self.streams[e]:
                for d in o.deps:
                    if not d.is_dma:
                        if d.eng == "pe" and o.eng == "pe" and not o.is_dma:
                            continue
                        d.milestone = True
        for e in ENGINES:
            for o in self.streams[e]:
                if o.milestone and not o.is_dma:
                    self.eng_counts[e] += 1
                    o.count = self.eng_counts[e]
        for k in self.dma_sem_counts:
            if k not in self.dma_sems:
                self.dma_sems[k] = self.stack.enter_context(nc.semaphore("dsem_" + k))
        eng_sem, dma_sem = self.eng_sems, self.dma_sems
        final = {}
        for e in ENGINES:
            for o in self.streams[e]:
                if o.is_dma and o.out_dram:
                    final[o.sem] = max(final.get(o.sem, 0), o.count)
        streams = self.streams

        def run(ename, eng):
            waited = {}
            for o in streams[ename]:
                need = {}
                for d in o.deps:
                    if d.is_dma:
                        key = ("d", d.sem)
                        s = dma_sem[d.sem]
                    else:
                        if d.eng == "pe" and ename == "pe" and not o.is_dma:
                            continue
                        key = ("e", d.eng)
                        s = eng_sem[d.eng]
                    if d.count > need.get(key, (None, 0))[1]:
                        need[key] = (s, d.count)
                for key, (s, c) in need.items():
                    if waited.get(key, 0) >= c:
                        continue
                    eng.wait_ge(s, c)
                    waited[key] = c
                inst = o.fn(eng)
                if o.is_dma:
                    if o.sem.startswith("cc"):
                        inst.then_inc(dma_sem[o.sem])
                    else:
                        inst.then_inc(dma_sem[o.sem], 16)
                elif o.milestone:
                    inst.then_inc(eng_sem[ename], 1)
            if ename == "sp":
                for k, c in final.items():
                    eng.wait_ge(dma_sem[k], c)

        with nc.Block(no_gpsimd_drain=True) as block:
            @block.tensor
            def _(eng):
                run("pe", eng)

            @block.scalar
            def _(eng):
                run("act", eng)

            @block.vector
            def _(eng):
                run("dve", eng)

            @block.gpsimd
            def _(eng):
                run("pool", eng)

            @block.sync
            def _(eng):
                run("sp", eng)

        self.streams = {e: [] for e in ENGINES}
        self.bufs = {}
        self.phase_sem_map = {}


def build_F(nc, S, stack, pfx, x_in, x_out, xT_out, wg_d, wu_d, wd_d, lng_d, lnb_d, ident_d, ps, ntiles=NT):
    sb = lambda name, shape, dt: stack.enter_context(nc.sbuf_tensor(pfx + name, shape, dt))
    nblk = ntiles // 2
    wg = sb("wg", [128, 8, DFF], BF16)
    wu = sb("wu", [128, 8, DFF], BF16)
    wd = sb("wd", [128, NFC, D], BF16)
    lng = sb("lng", [128, D], F32)
    lnb = sb("lnb", [128, D], F32)
    ident = sb("ident", [128, 128], F32)
    xin = sb("xin", [128, 2, 2, D], F32)
    xT = sb("xT", [128, 2, 8, 256], BF16)
    sg = sb("sg", [128, 2, 256], F32)
    actT = sb("actT", [128, 2, 256], BF16)
    st6 = sb("st6", [128, 2, 12], F32)
    mv = sb("mv", [128, 2, 2], F32)
    ve = sb("ve", [128, 2, 1], F32)
    rstd = sb("rstd", [128, 2, 1], F32)
    mhalf = sb("mhalf", [128, 1], F32)
    xoT = sb("xoT", [128, 8, 256], BF16) if xT_out is not None else None

    B = lambda n: S.buf(pfx + n)
    wg_v = wg_d.rearrange("(c p) f -> p c f", p=128)
    wu_v = wu_d.rearrange("(c p) f -> p c f", p=128)
    wd_v = wd_d.rearrange("(c p) f -> p c f", p=128)
    GB = [0, 6, 12, 17, NFC]
    grp = lambda fc: 0 if fc < 6 else (1 if fc < 12 else (2 if fc < 17 else 3))
    for g in range(4):
        lo, hi = GB[g], GB[g + 1]
        S.dma("pool", lambda e, lo=lo, hi=hi: e.dma_start(out=wg[:, :, lo * 128:hi * 128], in_=wg_v[:, :, lo * 128:hi * 128]),
              None, writes=[B("wg%d" % g)])
        S.dma("pool", lambda e, lo=lo, hi=hi: e.dma_start(out=wu[:, :, lo * 128:hi * 128], in_=wu_v[:, :, lo * 128:hi * 128]),
              None, writes=[B("wu%d" % g)])
        S.dma("pool", lambda e, lo=lo, hi=hi: e.dma_start(out=wd[:, lo:hi, :], in_=wd_v[:, lo:hi, :]),
              None, writes=[B("wd%d" % g)])
    S.dma("sp", lambda e: e.dma_start(out=lng[:], in_=lng_d.partition_broadcast(128)), pfx + "c", writes=[B("lng")])
    S.dma("sp", lambda e: e.dma_start(out=lnb[:], in_=lnb_d.partition_broadcast(128)), pfx + "c", writes=[B("lnb")])
    S.dma("sp", lambda e: e.dma_start(out=ident[:], in_=ident_d), pfx + "c", writes=[B("ident")])
    S.op("pool", lambda e: e.memset(mhalf[:], -0.5), writes=[B("mhalf")])

    xin_v = x_in.rearrange("(b t p) d -> b p t d", t=2, p=128)
    xout_v = x_out.rearrange("(b t p) d -> b p t d", t=2, p=128)
    xTo_v = xT_out.rearrange("(c p) t -> p c t", p=128) if xT_out is not None else None

    acc = lambda j, hd: ps[:, 2 * j + hd, :]
    TP = ps[:, 6:8, :].rearrange("p b (c t) -> p (b c) t", t=128)

    def load(b):
        sl = b % 2
        S.dma("sp", lambda e: e.dma_start(out=xin[:, sl, :, :], in_=xin_v[b]), pfx + "xin%d" % sl,
              writes=[B("xin%d_0" % sl), B("xin%d_1" % sl)])

    def tr_in(b, j):
        sl = b % 2
        for dc in range(8):
            S.op("pe", lambda e, dc=dc: e.transpose(out=TP[:, dc, :], in_=xin[:, sl, j, dc * 128:(dc + 1) * 128], identity=ident[:]),
                 reads=[B("xin%d_%d" % (sl, j)), B("ident")], writes=[B("TP")])
        S.op("act", lambda e: e.copy(out=xT[:, sl, :, j * 128:(j + 1) * 128], in_=TP),
             reads=[B("TP")], writes=[B("xT%d" % sl)])

    def up(b, fc):
        sl = b % 2
        par = fc % 2
        gb = ps[:, 4 + par, 0:256]
        ub = ps[:, 4 + par, 256:512]
        for dc in range(8):
            S.op("pe", lambda e, dc=dc: e.matmul(gb, lhsT=wg[:, dc, fc * 128:(fc + 1) * 128], rhs=xT[:, sl, dc, :],
                                                 start=(dc == 0), stop=(dc == 7)),
                 reads=[B("wg%d" % grp(fc)), B("xT%d" % sl)], writes=[B("gu%d" % par)])
        for dc in range(8):
            S.op("pe", lambda e, dc=dc: e.matmul(ub, lhsT=wu[:, dc, fc * 128:(fc + 1) * 128], rhs=xT[:, sl, dc, :],
                                                 start=(dc == 0), stop=(dc == 7)),
                 reads=[B("wu%d" % grp(fc)), B("xT%d" % sl)], writes=[B("gu%d" % par)])
        S.op("act", lambda e: e.activation(out=sg[:, par, :], in_=gb, func=AF.Silu),
             reads=[B("gu%d" % par)], writes=[B("sg%d" % par)])
        S.op("dve", lambda e: e.scalar_tensor_tensor(out=actT[:, par, :], in0=sg[:, par, :], scalar=0.5, in1=ub,
                                                     op0=ALU.mult, op1=ALU.mult),
             reads=[B("sg%d" % par), B("gu%d" % par)], writes=[B("actT%d" % par)])

    def down(b, fc):
        par = fc % 2
        for j in range(2):
            for hd in range(2):
                S.op("pe", lambda e, j=j, hd=hd: e.matmul(acc(j, hd), lhsT=actT[:, par, j * 128:(j + 1) * 128],
                                                          rhs=wd[:, fc, hd * 512:(hd + 1) * 512],
                                                          start=(fc == 0), stop=(fc == NFC - 1)),
                     reads=[B("actT%d" % par), B("wd%d" % grp(fc))], writes=[B("acc%d" % j)])

    def epi_steps(b, j):
        sl = b % 2
        xb = B("xin%d_%d" % (sl, j))
        xt = lambda lo, hi: xin[:, sl, j, lo:hi]

        def s_pre():
            for hd in range(2):
                S.op("dve", lambda e, hd=hd: e.scalar_tensor_tensor(out=xt(hd * 512, (hd + 1) * 512), in0=xt(hd * 512, (hd + 1) * 512),
                                                                    scalar=ALPHA, in1=acc(j, hd), op0=ALU.mult, op1=ALU.add),
                     reads=[B("acc%d" % j), xb], writes=[xb])

        def s_stats():
            for hd in range(2):
                S.op("dve", lambda e, hd=hd: e.bn_stats(out=st6[:, j, hd * 6:(hd + 1) * 6], in_=xt(hd * 512, (hd + 1) * 512)),
                     reads=[xb], writes=[B("st6_%d" % j)])
            S.op("dve", lambda e: e.bn_aggr(out=mv[:, j, :], in_=st6[:, j, :]), reads=[B("st6_%d" % j)], writes=[B("mv%d" % j)])

        def s_rstd():
            S.op("pool", lambda e: e.tensor_scalar(out=ve[:, j, :], in0=mv[:, j, 1:2], scalar1=LN_EPS, scalar2=None, op0=ALU.add),
                 reads=[B("mv%d" % j)], writes=[B("ve%d" % j)])
            S.op("pool", lambda e: e.tensor_tensor(out=rstd[:, j, :], in0=ve[:, j, :], in1=mhalf[:], op=ALU.pow),
                 reads=[B("ve%d" % j), B("mhalf")], writes=[B("rstd%d" % j)])

        def s_norm1():
            S.op("dve", lambda e: e.scalar_tensor_tensor(out=xt(0, D), in0=xt(0, D), scalar=mv[:, j, 0:1], in1=lng[:],
                                                         op0=ALU.subtract, op1=ALU.mult),
                 reads=[xb, B("mv%d" % j), B("lng")], writes=[xb])

        def s_norm2():
            S.op("dve", lambda e: e.scalar_tensor_tensor(out=xt(0, D), in0=xt(0, D), scalar=rstd[:, j, :], in1=lnb[:],
                                                         op0=ALU.mult, op1=ALU.add),
                 reads=[xb, B("rstd%d" % j), B("lnb")], writes=[xb])

        return [s_pre, s_stats, s_rstd, s_norm1, s_norm2]

    def store(b):
        sl = b % 2
        S.dma("sp", lambda e: e.dma_start(out=xout_v[b], in_=xin[:, sl, :, :]), pfx + "xo%d" % sl,
              reads=[B("xin%d_0" % sl), B("xin%d_1" % sl)], out_dram=True)

    def tr_out(b, j):
        sl = b % 2
        for dc in range(8):
            S.op("pe", lambda e, dc=dc: e.transpose(out=TP[:, dc, :], in_=xin[:, sl, j, dc * 128:(dc + 1) * 128], identity=ident[:]),
                 reads=[B("xin%d_%d" % (sl, j)), B("ident")], writes=[B("TP")])
        S.op("act", lambda e: e.copy(out=xoT[:, :, j * 128:(j + 1) * 128], in_=TP),
             reads=[B("TP")], writes=[B("xoT")])

    def store_T(b):
        S.dma("sp", lambda e: e.dma_start(out=xTo_v[:, :, b * 256:(b + 1) * 256], in_=xoT[:]), pfx + "xTo",
              reads=[B("xoT")], out_dram=True)

    load(0)
    if nblk > 1:
        load(1)
    tr_in(0, 0)
    tr_in(0, 1)
    pending = []
    for b in range(nblk):
        up(b, 0)
        for fc in range(NFC):
            if fc + 1 < NFC:
                up(b, fc + 1)
            down(b, fc)
            if pending:
                pending.pop(0)()
            if b > 0:
                if xT_out is not None:
                    if fc == 10:
                        tr_out(b - 1, 0)
                    if fc == 11:
                        tr_out(b - 1, 1)
                        store_T(b - 1)
                if fc == 12:
                    store(b - 1)
                    if b + 1 < nblk:
                        load(b + 1)
            if b + 1 < nblk:
                if fc == 17:
                    tr_in(b + 1, 0)
                if fc == 19:
                    tr_in(b + 1, 1)
        e0 = epi_steps(b, 0)
        e1 = epi_steps(b, 1)
        e0[0]()
        e1[0]()
        pending = [s for pair in zip(e0[1:], e1[1:]) for s in pair]
    for s in pending:
        s()
    if xT_out is not None:
        tr_out(nblk - 1, 0)
        tr_out(nblk - 1, 1)
        store_T(nblk - 1)
    store(nblk - 1)


def build_P(nc, S, stack, pfx, xT_d, win_d, cs_d, qT_d, kT_d, vaug_d, ab_d, ps, ntok=TOK, xch=None):
    sb = lambda name, shape, dt: stack.enter_context(nc.sbuf_tensor(pfx + name, shape, dt))
    B = lambda n: S.buf(pfx + n)
    nblk = ntok // 512
    win = sb("win", [128, 8, 2048], BF16)
    cs = sb("cs", [128, 256], BF16)
    xT = sb("xT", [128, 2, 8, 512], BF16)
    uT = sb("uT", [128, 4, 512], BF16)
    qkst = sb("qkst", [128, 2, 512], BF16)
    vst = sb("vst", [128, 2, 8, 65], BF16)
    abst = sb("abst", [128, 2, 1024], BF16)

    win_v = win_d.rearrange("(c p) f -> p c f", p=128)
    for hh in range(2):
        S.dma("pool", lambda e, hh=hh: e.dma_start(out=win[:, :, hh * 1024:(hh + 1) * 1024], in_=win_v[:, :, hh * 1024:(hh + 1) * 1024]),
              pfx + "w", writes=[B("win")])
    S.dma("sp", lambda e: e.dma_start(out=cs[:], in_=cs_d), pfx + "c", writes=[B("cs")])
    S.op("dve", lambda e: e.memset(vst[:], 1.0), writes=[B("vst0"), B("vst1")])
    xT_v = xT_d.rearrange("(c p) t -> p c t", p=128)
    vaug_v = vaug_d.rearrange("(n p) f -> n p f", p=128)
    ab_v = ab_d.rearrange("(n p) f -> n p f", p=128)

    bank_i = [0]

    def bank():
        i = bank_i[0] % 8
        bank_i[0] += 1
        return i

    cnt = {"qk": 0, "v": 0, "ab": 0}

    def load(b):
        sl = b % 2
        S.dma("sp", lambda e: e.dma_start(out=xT[:, sl, :, :], in_=xT_v[:, :, b * 512:(b + 1) * 512]), pfx + "x%d" % sl,
              writes=[B("xT%d" % sl)])

    load(0)
    for b in range(nblk):
        sl = b % 2
        if b + 1 < nblk:
            load(b + 1)
        for oc in range(8):
            bk = bank()
            for dc in range(8):
                S.op("pe", lambda e, dc=dc, oc=oc, bk=bk, sl=sl: e.matmul(ps[:, bk, :], lhsT=win[:, dc, oc * 128:(oc + 1) * 128],
                                                                          rhs=xT[:, sl, dc, :], start=(dc == 0), stop=(dc == 7)),
                     reads=[B("win"), B("xT%d" % sl)], writes=[B("bank%d" % bk)])
            qs = cnt["qk"] % 2
            cnt["qk"] += 1
            if oc < 4:
                S.op("act", lambda e, bk=bk, qs=qs: e.mul(out=qkst[:, qs, :], in_=ps[:, bk, :], mul=0.125),
                     reads=[B("bank%d" % bk)], writes=[B("qkst%d" % qs)])
                dst = qT_d[oc * 128:(oc + 1) * 128, b * 512:(b + 1) * 512]
            else:
                S.op("act", lambda e, bk=bk, qs=qs: e.copy(out=qkst[:, qs, :], in_=ps[:, bk, :]),
                     reads=[B("bank%d" % bk)], writes=[B("qkst%d" % qs)])
                dst = kT_d[(oc - 4) * 128:(oc - 3) * 128, b * 512:(b + 1) * 512]
            S.dma("sp", lambda e, dst=dst, qs=qs: e.dma_start(out=dst, in_=qkst[:, qs, :]), pfx + "qk%d" % qs,
                  reads=[B("qkst%d" % qs)], writes=([B("kTd%d" % b)] if oc >= 4 else []), out_dram=True)
        for g in range(4):
            bk = bank()
            for dc in range(8):
                S.op("pe", lambda e, dc=dc, g=g, bk=bk, sl=sl: e.matmul(ps[:, bk, :], lhsT=win[:, dc, 1536 + g * 128:1536 + (g + 1) * 128],
                                                                        rhs=xT[:, sl, dc, :], start=(dc == 0), stop=(dc == 7)),
                     reads=[B("win"), B("xT%d" % sl)], writes=[B("bank%d" % bk)])
            S.op("dve", lambda e, bk=bk, g=g: e.tensor_copy(out=uT[:, g, :], in_=ps[:, bk, :]),
                 reads=[B("bank%d" % bk)], writes=[B("uT")])
        for j in range(4):
            tile_i = b * 4 + j
            bk = bank()
            for dc in range(8):
                S.op("pe", lambda e, dc=dc, bk=bk, sl=sl, j=j: e.matmul(ps[:, bk, :], lhsT=xT[:, sl, dc, j * 128:(j + 1) * 128],
                                                                        rhs=win[:, dc, 1024:1536], start=(dc == 0), stop=(dc == 7)),
                     reads=[B("win"), B("xT%d" % sl)], writes=[B("bank%d" % bk)])
            vs = cnt["v"] % 2
            cnt["v"] += 1
            S.op("dve", lambda e, bk=bk, vs=vs: e.tensor_copy(out=vst[:, vs, :, 0:64], in_=ps[:, bk, :].rearrange("p (h d) -> p h d", d=64)),
                 reads=[B("bank%d" % bk)], writes=[B("vst%d" % vs)])
            S.dma("sp", lambda e, vs=vs, tile_i=tile_i: e.dma_start(out=vaug_v[tile_i], in_=vst[:, vs, :, :].rearrange("p h d -> p (h d)")),
                  pfx + "v%d" % vs, reads=[B("vst%d" % vs)], writes=[B("vd%d" % b)], out_dram=True)
            bka = bank()
            bkb = bank()
            for g in range(4):
                S.op("pe", lambda e, g=g, bka=bka, j=j: e.matmul(ps[:, bka, g * 128:(g + 1) * 128], lhsT=uT[:, g, j * 128:(j + 1) * 128],
                                                                 rhs=cs[:, 0:128], start=True, stop=True),
                     reads=[B("uT"), B("cs")], writes=[B("bank%d" % bka)])
                S.op("pe", lambda e, g=g, bkb=bkb, j=j: e.matmul(ps[:, bkb, g * 128:(g + 1) * 128], lhsT=uT[:, g, j * 128:(j + 1) * 128],
                                                                 rhs=cs[:, 128:256], start=True, stop=True),
                     reads=[B("uT"), B("cs")], writes=[B("bank%d" % bkb)])
            asl = cnt["ab"] % 2
            cnt["ab"] += 1
            S.op("act", lambda e, bka=bka, asl=asl: e.copy(out=abst[:, asl, 0:512], in_=ps[:, bka, :]),
                 reads=[B("bank%d" % bka)], writes=[B("abst%d" % asl)])
            S.op("dve", lambda e, bkb=bkb, asl=asl: e.tensor_copy(out=abst[:, asl, 512:1024], in_=ps[:, bkb, :]),
                 reads=[B("bank%d" % bkb)], writes=[B("abst%d" % asl)])
            S.dma("sp", lambda e, asl=asl, tile_i=tile_i: e.dma_start(out=ab_v[tile_i], in_=abst[:, asl, :]), pfx + "ab%d" % asl,
                  reads=[B("abst%d" % asl)], writes=[B("abd%d" % b)], out_dram=True)
        if xch is not None:
            _exchange_block(S, B, b, nblk, kT_d, vaug_d, ab_d, xch)


def build_MA(nc, S, stack, pfx, ab_all_d, MA_d, T_d, ps, MA_pre=None):
    sb = lambda name, shape, dt: stack.enter_context(nc.sbuf_tensor(pfx + name, shape, dt))
    B = lambda n: S.buf(pfx + n)
    CH = 16
    ZA = sb("ZA", [128, 2, CH, 512], BF16)
    TA = sb("TA", [128, 2, CH, 512], BF16)
    if MA_pre is None:
        MA = sb("MA", [128, 128, 128], BF16)
        S.dma("sp", lambda e: e.dma_start(out=MA[:].rearrange("p s m -> p (s m)"), in_=MA_d), pfx + "c", writes=[B("MA")])
    else:
        MA = MA_pre
    abv = ab_all_d.rearrange("(s1 s2) (r c) -> r s1 s2 c", s2=128, r=2)

    def load(c):
        sl = c % 2
        for ri in range(2):
            S.dma("sp", lambda e, ri=ri: e.dma_start(out=ZA[ri * 64:(ri + 1) * 64, sl, :, :], in_=abv[ri][:, c * CH:(c + 1) * CH, :]),
                  pfx + "z%d" % sl, writes=[B("ZA%d" % sl)])

    load(0)
    cnt = [0]

    def chunk(c):
        sl = c % 2
        if c + 1 < 128 // CH:
            load(c + 1)
        for s2l in range(CH):
            s2 = c * CH + s2l
            n = cnt[0]
            cnt[0] += 1
            bk = n % 8
            S.op("pe", lambda e, s2=s2, s2l=s2l, bk=bk: e.matmul(ps[:, bk, :], lhsT=MA[:, s2, :], rhs=ZA[:, sl, s2l, :], start=True, stop=True),
                 reads=[B("MA"), B("ZA%d" % sl)], writes=[B("bank%d" % bk)])
            if n % 2 == 0:
                S.op("act", lambda e, s2l=s2l, bk=bk: e.copy(out=TA[:, sl, s2l, :], in_=ps[:, bk, :]),
                     reads=[B("bank%d" % bk)], writes=[B("TA%d" % sl)])
            else:
                S.op("dve", lambda e, s2l=s2l, bk=bk: e.tensor_copy(out=TA[:, sl, s2l, :], in_=ps[:, bk, :]),
                     reads=[B("bank%d" % bk)], writes=[B("TA%d" % sl)])
        S.dma("sp", lambda e: e.dma_start(out=T_d[:, c * CH:(c + 1) * CH, :], in_=TA[:, sl, :, :]), pfx + "t%d" % sl,
              reads=[B("TA%d" % sl)], out_dram=True)

    for c in range(128 // CH):
        chunk(c)


def build_MB(nc, S, stack, pfx, T_d, CB_d, F_d, rinvf, ps):
    sb = lambda name, shape, dt: stack.enter_context(nc.sbuf_tensor(pfx + name, shape, dt))
    B = lambda n: S.buf(pfx + n)
    CB = sb("CB", [128, 2, 64], BF16)
    ones = sb("ones", [128, 1], F32)
    TB = sb("TB", [128, 2, 2, 8, 512], BF16)
    fourT = sb("fourT", [128, 4, TOK], BF16)
    sq = sb("sq", [128, 2, 512], F32)
    ssq = sb("ssq", [1, TOK], F32)
    ve = sb("ve", [128, NT], F32)
    mh = sb("mh", [128, NT], F32)
    S.dma("sp", lambda e: e.dma_start(out=CB[:].rearrange("p a b -> p (a b)"), in_=CB_d), pfx + "c", writes=[B("CB")])
    S.op("dve", lambda e: e.memset(ones[:], 1.0), writes=[B("ones")])
    S.op("pool", lambda e: e.memset(mh[:], -0.5), writes=[B("mh")])
    Tv = T_d.rearrange("(r k) s c -> s r k c", r=2)
    f4 = fourT[:].rearrange("p c (a b) -> p c a b", b=64)
    ssq3 = ssq[:].rearrange("p (a b) -> p a b", b=64)

    def load(kc):
        sl = kc % 2
        for ri in range(2):
            S.dma("sp", lambda e, ri=ri: e.dma_start(out=TB[:, sl, ri, :, :], in_=Tv[:, ri, kc * 8:(kc + 1) * 8, :]), pfx + "tb%d" % sl,
                  writes=[B("TB%d" % sl)])

    load(0)
    cnt = [0]

    def kchunk(kc):
        sl = kc % 2
        if kc + 1 < 8:
            load(kc + 1)
        bs = 6 + kc % 2
        for cb in range(4):
            n = cnt[0]
            cnt[0] += 1
            bk = n % 6
            for k1l in range(8):
                for ri in range(2):
                    S.op("pe", lambda e, k1l=k1l, ri=ri, cb=cb, bk=bk: e.matmul(
                        ps[:, bk, k1l * 64:(k1l + 1) * 64], lhsT=TB[:, sl, ri, k1l, cb * 128:(cb + 1) * 128], rhs=CB[:, ri, :],
                        start=(ri == 0), stop=(ri == 1)),
                        reads=[B("TB%d" % sl), B("CB")], writes=[B("bank%d" % bk)])
            S.op("act", lambda e, cb=cb, bk=bk: e.copy(
                out=f4[:, cb, :, kc * 8:(kc + 1) * 8].rearrange("p a b -> p b a"),
                in_=ps[:, bk, :].rearrange("p (b a) -> p b a", a=64)),
                reads=[B("bank%d" % bk)], writes=[B("fourT")])
            sqs = n % 2
            S.op("act", lambda e, bk=bk, sqs=sqs: e.activation(out=sq[:, sqs, :], in_=ps[:, bk, :], func=AF.Square),
                 reads=[B("bank%d" % bk)], writes=[B("sq%d" % sqs)])
            S.op("pe", lambda e, cb=cb, bs=bs, sqs=sqs: e.matmul(ps[0:1, bs, :], lhsT=ones[:, 0:1], rhs=sq[:, sqs, :],
                                                                start=(cb == 0), stop=(cb == 3)),
                 reads=[B("sq%d" % sqs), B("ones")], writes=[B("bank%d" % bs)])
        S.op("dve", lambda e, bs=bs: e.tensor_copy(out=ssq3[:, :, kc * 8:(kc + 1) * 8].rearrange("p a b -> p b a"),
                                                   in_=ps[0:1, bs, :].rearrange("p (b a) -> p b a", a=64)),
             reads=[B("bank%d" % bs)], writes=[B("ssq")])

    for kc in range(8):
        kchunk(kc)
    for i in range(NT):
        S.op("pe", lambda e, i=i: e.matmul(ps[:, 0, i:i + 1], lhsT=ssq[0:1, i * 128:(i + 1) * 128], rhs=ones[0:1, 0:1],
                                           start=True, stop=True),
             reads=[B("ssq"), B("ones")], writes=[B("bank0")])
    S.op("dve", lambda e: e.tensor_scalar(out=ve[:], in0=ps[:, 0, 0:NT], scalar1=1.0 / 512, scalar2=RMS_EPS, op0=ALU.mult, op1=ALU.add),
         reads=[B("bank0")], writes=[B("ve")])
    S.op("pool", lambda e: e.tensor_tensor(out=rinvf[:], in0=ve[:], in1=mh[:], op=ALU.pow),
         reads=[B("ve"), B("mh")], writes=[S.buf("rinvf")])
    S.dma("sp", lambda e: e.dma_start(out=F_d.rearrange("c p t -> p c t"), in_=fourT[:]), pfx + "f",
          reads=[B("fourT")], out_dram=True)


def load_qk(S, qT, kT, qT_d, kT_d, kg_d, out_dram=False):
    kr = lambda ap: ap.rearrange("(c p) t -> p c t", p=128)
    S.dma("sp", lambda e: e.dma_start(out=qT[:], in_=kr(qT_d)), None, writes=[S.buf("qT_res")], out_dram=out_dram)
    if kg_d is None:
        S.dma("sp", lambda e: e.dma_start(out=kT[:], in_=kr(kT_d)), None, writes=[S.buf("kT_res")], out_dram=out_dram)
    else:
        S.dma("sp", lambda e: e.dma_start(out=kT[:, :, 0:256], in_=kr(kg_d[0:512, 256:512])), None, writes=[S.buf("kT_res")], out_dram=out_dram)
        S.dma("sp", lambda e: e.dma_start(out=kT[:, :, 256:256 + TOK], in_=kr(kT_d)), None, writes=[S.buf("kT_res")], out_dram=out_dram)
        S.dma("sp", lambda e: e.dma_start(out=kT[:, :, 256 + TOK:512 + TOK], in_=kr(kg_d[512:1024, 0:256])), None,
              writes=[S.buf("kT_res")], out_dram=out_dram)


def build_MC(nc, S, stack, pfx, x1_d, x2_d, qT_d, kTh_d, vh_d, F_d, rinvf, bint_d, bsp_d, wout_d, ga_d, gf_d,
             lng_d, lnb_d, ident_d, ps, dbg_tiles=None, dbg_stage=99, kg_d=None, vg_d=None, qk_pre=None):
    sb = lambda name, shape, dt: stack.enter_context(nc.sbuf_tensor(pfx + name, shape, dt))
    B = lambda n: S.buf(pfx + n)
    if qk_pre is None:
        qT = sb("qT", [128, 4, TOK], BF16)
        kT = sb("kT", [128, 4, 72 * 64], BF16)
    else:
        qT, kT = qk_pre
    va = sb("va", [128, 36, 520], BF16)
    bint = sb("bint", [128, 8, 5, 128], BF16)
    bsp = sb("bsp", [128, 8, 6, 128], BF16)
    wo = sb("wo", [128, 8, D], BF16)
    wst = sb("wst", [128, 2, D], F32)
    gcol = sb("gcol", [128, 8], F32)
    lng = sb("lng", [128, D], F32)
    lnb = sb("lnb", [128, D], F32)
    ident = sb("ident", [128, 128], F32)
    identb = sb("identb", [128, 128], BF16)
    xt = sb("xt", [128, 2, D], F32)
    ft = sb("ft", [128, 2, 4, 128], BF16)
    PT = sb("PT", [128, 2, 6, 128], BF16)
    osb = sb("osb", [128, 8, 64], F32)
    junk = sb("junk", [128, 512], F32)
    rden = sb("rden", [128, 8], F32)
    ssqa = sb("ssqa", [128, 1], F32)
    vea = sb("vea", [128, 1], F32)
    rinva = sb("rinva", [128, 1], F32)
    attnT = sb("attnT", [128, 4, 128], BF16)
    st6 = sb("st6", [128, 12], F32)
    mv = sb("mv", [128, 2], F32)
    ve = sb("ve", [128, 1], F32)
    rstd = sb("rstd", [128, 1], F32)
    mhalf = sb("mhalf", [128, 1], F32)

    if qk_pre is None:
        load_qk(S, qT, kT, qT_d, kTh_d, kg_d)
    if kg_d is None:
        S.dma("sp", lambda e: e.dma_start(out=va[:], in_=vh_d.rearrange("(t p) f -> p t f", p=128)), pfx + "c", writes=[B("va")])
    else:
        kr = lambda ap: ap.rearrange("(c p) t -> p c t", p=128)
        vr = lambda ap: ap.rearrange("(t p) f -> p t f", p=128)
        S.dma("sp", lambda e: e.dma_start(out=va[:, 0:2, :], in_=vr(vg_d[256:512, :])), pfx + "c", writes=[B("va")])
        S.dma("sp", lambda e: e.dma_start(out=va[:, 2:34, :], in_=vr(vh_d)), pfx + "c", writes=[B("va")])
        S.dma("sp", lambda e: e.dma_start(out=va[:, 34:36, :], in_=vr(vg_d[512:768, :])), pfx + "c", writes=[B("va")])
    bint_f = bint[:].rearrange("p h t q -> p (h t q)")
    for k in range(4):
        S.dma("pool", lambda e, k=k: e.dma_start(out=bint_f[:, k * 1280:(k + 1) * 1280], in_=bint_d[:, k * 1280:(k + 1) * 1280]),
              pfx + "cb", writes=[B("bint")])
    S.dma("sp", lambda e: e.dma_start(out=lng[:], in_=lng_d.partition_broadcast(128)), pfx + "c", writes=[B("lng")])
    S.dma("sp", lambda e: e.dma_start(out=lnb[:], in_=lnb_d.partition_broadcast(128)), pfx + "c", writes=[B("lnb")])
    S.dma("sp", lambda e: e.dma_start(out=ident[:], in_=ident_d), pfx + "c", writes=[B("ident")])
    S.dma("pool", lambda e: e.dma_start(out=identb[:], in_=ident_d), pfx + "cb", writes=[B("identb")])
    S.dma("sp", lambda e: e.dma_start(out=gcol[:, 0:4], in_=ga_d.rearrange("(c p) -> p c", p=128), allow_slow_non_contiguous=True), pfx + "c", writes=[B("gcol")])
    S.dma("sp", lambda e: e.dma_start(out=gcol[:, 4:8], in_=gf_d.rearrange("(c p) -> p c", p=128), allow_slow_non_contiguous=True), pfx + "c", writes=[B("gcol")])
    S.op("pool", lambda e: e.memset(mhalf[:], -0.5), writes=[B("mhalf")])
    for c in range(8):
        ws = c % 2
        S.dma("sp", lambda e, c=c, ws=ws: e.dma_start(out=wst[:, ws, :], in_=wout_d[c * 128:(c + 1) * 128, :]), pfx + "w%d" % ws,
              writes=[B("wst%d" % ws)])
        S.op("dve", lambda e, c=c, ws=ws: e.tensor_scalar(out=wo[:, c, :], in0=wst[:, ws, :], scalar1=gcol[:, c:c + 1], scalar2=None,
                                                          op0=ALU.mult),
             reads=[B("wst%d" % ws), B("gcol")], writes=[B("wo")])

    SPECIAL = {0: 0, 1: 1, NT - 2: 2, NT - 1: 3}
    x1v = x1_d.rearrange("(n p) d -> n p d", p=128)
    x2v = x2_d.rearrange("(n p) d -> n p d", p=128)
    Fv = F_d.rearrange("c p t -> p c t")
    Sflat = lambda sl: ps[:, 2 * sl:2 * sl + 2, :].rearrange("p b c -> p (b c)")
    O4 = ps[:, 4:6, 0:260].rearrange("p b (h d) -> p b h d", d=65)
    TR = ps[:, 4, :].rearrange("p (c t) -> p c t", t=128)

    def load_tile(i):
        sl = i % 2
        S.dma("sp", lambda e: e.dma_start(out=xt[:, sl, :], in_=x1v[i]), pfx + "x%d" % sl, writes=[B("xt%d" % sl)])
        S.dma("sp", lambda e: e.dma_start(out=ft[:, sl, :, :], in_=Fv[:, :, i * 128:(i + 1) * 128]), pfx + "x%d" % sl,
              writes=[B("ft%d" % sl)])

    def load_bsp(v):
        bsp_f = bsp[:].rearrange("p h t q -> p (h t q)")
        for k in range(4):
            S.dma("pool", lambda e, k=k: e.dma_start(out=bsp_f[:, k * 1536:(k + 1) * 1536], in_=bsp_d[v][:, k * 1536:(k + 1) * 1536]),
                  pfx + "bs", writes=[B("bsp")])

    TRb = ps[:, 7, :].rearrange("p (c t) -> p c t", t=128)
    ntl = dbg_tiles if dbg_tiles is not None else NT

    def geom(i):
        t0, nkt = i, 5
        if i == 0:
            nkt = 6
        if i == NT - 1:
            t0, nkt = NT - 2, 6
        if i in SPECIAL:
            return t0, nkt, bsp, "bsp"
        return t0, nkt, bint, "bint"

    def head_S(i, h, ss):
        t0, nkt, btab, bname = geom(i)
        hc, po = h // 2, (h % 2) * 64
        Sf = Sflat(ss)
        sbanks = [B("bank%d" % (2 * ss)), B("bank%d" % (2 * ss + 1))]
        S.op("pe", lambda e: e.matmul(Sf[:, 0:512], lhsT=identb[:], rhs=btab[:, h, 0:4, :].rearrange("p t q -> p (t q)"),
                                      start=True, stop=False),
             reads=[B("identb"), B(bname)], writes=[sbanks[0]])
        S.op("pe", lambda e: e.matmul(Sf[:, 512:nkt * 128], lhsT=identb[:], rhs=btab[:, h, 4:nkt, :].rearrange("p t q -> p (t q)"),
                                      start=True, stop=False),
             reads=[B("identb"), B(bname)], writes=[sbanks[1]])
        for tt in range(nkt):
            S.op("pe", lambda e, tt=tt: e.matmul(
                Sf[:, tt * 128:(tt + 1) * 128], lhsT=kT[po:po + 64, hc, (t0 + tt) * 128:(t0 + tt + 1) * 128],
                rhs=qT[po:po + 64, hc, i * 128:(i + 1) * 128], start=False, stop=(tt == 3 or tt == nkt - 1)),
                reads=[S.buf("kT_res"), S.buf("qT_res")], writes=[sbanks[tt // 4]])
        S.op("act", lambda e: e.activation(out=PT[:, ss, 0:nkt, :].rearrange("p t q -> p (t q)"), in_=Sf[:, 0:nkt * 128], func=AF.Exp),
             reads=sbanks, writes=[B("PT%d" % ss)])

    def head_PV(i, h, ss):
        t0, nkt, btab, bname = geom(i)
        for tt in range(nkt):
            S.op("pe", lambda e, tt=tt: e.matmul(
                O4[:, h // 4, h % 4, :], lhsT=PT[:, ss, tt, :], rhs=va[:, t0 + tt, h * 65:(h + 1) * 65],
                start=(tt == 0), stop=(tt == nkt - 1)),
                reads=[B("PT%d" % ss), B("va")], writes=[B("bank%d" % (4 + h // 4))])

    def tail_parts(i):
        sl = i % 2
        xb = B("xt%d" % sl)
        X = lambda lo, hi: xt[:, sl, lo:hi]
        osf = osb[:].rearrange("p h d -> p (h d)")
        obanks = [B("bank4"), B("bank5")]

        def t_norm_half(b):
            S.op("dve", lambda e: e.reciprocal(out=rden[:, 4 * b:4 * b + 4], in_=O4[:, b, :, 64]),
                 reads=[obanks[b]], writes=[B("rden")])
            S.op("dve", lambda e: e.tensor_tensor(out=osb[:, 4 * b:4 * b + 4, :], in0=O4[:, b, :, 0:64],
                                                  in1=rden[:, 4 * b:4 * b + 4].to_broadcast([128, 4, 64]), op=ALU.mult),
                 reads=[obanks[b], B("rden")], writes=[B("osb")])

        def t_normA():
            t_norm_half(0)

        def t_norm():
            t_norm_half(1)
            S.op("dve", lambda e: e.scalar_tensor_tensor(out=junk[:], in0=osf, scalar=1.0, in1=osf, op0=ALU.mult, op1=ALU.mult,
                                                         accum_out=ssqa[:, 0:1]),
                 reads=[B("osb")], writes=[B("junk"), B("ssqa")])
            S.op("dve", lambda e: e.tensor_scalar(out=vea[:], in0=ssqa[:], scalar1=1.0 / 512, scalar2=RMS_EPS, op0=ALU.mult, op1=ALU.add),
                 reads=[B("ssqa")], writes=[B("vea")])
            S.op("pool", lambda e: e.tensor_tensor(out=rinva[:], in0=vea[:], in1=mhalf[:], op=ALU.pow),
                 reads=[B("vea"), B("mhalf")], writes=[B("rinva")])
            S.op("act", lambda e: e.mul(out=X(0, D), in_=X(0, D), mul=ALPHA), reads=[xb], writes=[xb])

        def t_tr():
            for c in range(4):
                S.op("pe", lambda e, c=c: e.transpose(out=TRb[:, c, :], in_=osf[:, c * 128:(c + 1) * 128], identity=ident[:]),
                     reads=[B("osb"), B("ident")], writes=[B("bank7")])
            S.op("act", lambda e: e.copy(out=attnT[:], in_=TRb), reads=[B("bank7")], writes=[B("attnT")])

        def t_w(hd):
            def f():
                lo, hi = hd * 512, (hd + 1) * 512
                for c in range(4):
                    S.op("pe", lambda e, c=c: e.matmul(ps[:, 6, :], lhsT=attnT[:, c, :], rhs=wo[:, c, lo:hi], start=(c == 0), stop=(c == 3)),
                         reads=[B("attnT"), B("wo")], writes=[B("bank6")])
                for c in range(4):
                    S.op("pe", lambda e, c=c: e.matmul(ps[:, 7, :], lhsT=ft[:, sl, c, :], rhs=wo[:, 4 + c, lo:hi], start=(c == 0), stop=(c == 3)),
                         reads=[B("ft%d" % sl), B("wo")], writes=[B("bank7")])
                S.op("dve", lambda e: e.scalar_tensor_tensor(out=X(lo, hi), in0=ps[:, 6, :], scalar=rinva[:, 0:1], in1=X(lo, hi),
                                                             op0=ALU.mult, op1=ALU.add),
                     reads=[B("bank6"), B("rinva"), xb], writes=[xb])
                S.op("dve", lambda e: e.scalar_tensor_tensor(out=X(lo, hi), in0=ps[:, 7, :], scalar=rinvf[:, i:i + 1], in1=X(lo, hi),
                                                             op0=ALU.mult, op1=ALU.add),
                     reads=[B("bank7"), S.buf("rinvf"), xb], writes=[xb])
            return f

        def t_ln():
            for hd in range(2):
                S.op("dve", lambda e, hd=hd: e.bn_stats(out=st6[:, hd * 6:(hd + 1) * 6], in_=X(hd * 512, (hd + 1) * 512)),
                     reads=[xb], writes=[B("st6")])
            S.op("dve", lambda e: e.bn_aggr(out=mv[:], in_=st6[:]), reads=[B("st6")], writes=[B("mv")])
            S.op("pool", lambda e: e.tensor_scalar(out=ve[:], in0=mv[:, 1:2], scalar1=LN_EPS, scalar2=None, op0=ALU.add),
                 reads=[B("mv")], writes=[B("ve")])
            S.op("pool", lambda e: e.tensor_tensor(out=rstd[:], in0=ve[:], in1=mhalf[:], op=ALU.pow),
                 reads=[B("ve"), B("mhalf")], writes=[B("rstd")])
            S.op("dve", lambda e: e.scalar_tensor_tensor(out=X(0, D), in0=X(0, D), scalar=mv[:, 0:1], in1=lng[:],
                                                         op0=ALU.subtract, op1=ALU.mult),
                 reads=[xb, B("mv"), B("lng")], writes=[xb])
            S.op("dve", lambda e: e.scalar_tensor_tensor(out=X(0, D), in0=X(0, D), scalar=rstd[:, 0:1], in1=lnb[:],
                                                         op0=ALU.mult, op1=ALU.add),
                 reads=[xb, B("rstd"), B("lnb")], writes=[xb])
            S.dma("sp", lambda e: e.dma_start(out=x2v[i], in_=xt[:, sl, :]), None, reads=[xb], out_dram=True)

        return [t_norm, t_tr, t_w(0), t_w(1), t_ln, t_normA]

    load_tile(0)
    load_bsp(0)
    seq = [(i, h) for i in range(ntl) for h in range(8)]
    pending = {}
    parts = tail_parts(0)
    head_S(0, 0, 0)
    for g, (i, h) in enumerate(seq):
        if g + 1 < len(seq):
            ni, nh = seq[g + 1]
            if nh == 0:
                if ni == 1:
                    load_bsp(1)
                elif ni == 2:
                    load_bsp(2)
                elif ni == NT - 1:
                    load_bsp(3)
            head_S(ni, nh, (g + 1) % 2)
        head_PV(i, h, g % 2)
        if h in pending:
            pending.pop(h)()
        if h == 3:
            parts[5]()
        if h == 6 and i + 1 < ntl:
            load_tile(i + 1)
        if h == 7:
            parts[0]()
            pending = {1: parts[1], 2: parts[2], 4: parts[3], 5: parts[4]}
            if i + 1 < ntl:
                parts = tail_parts(i + 1)
    for h in sorted(pending):
        pending[h]()


def host_consts(gathered_layout=True):
    n = np.arange(128)
    ang = 2 * np.pi * np.outer(n, n) / 128.0
    cs = (np.concatenate([np.cos(ang), np.sin(ang)], 1) / np.sqrt(128.0)).astype(ml_dtypes.bfloat16)
    s1 = np.arange(64)[:, None, None]
    s2 = np.arange(128)[None, :, None]
    k1 = np.arange(64)[None, None, :]
    th = 2 * np.pi * (s1 * k1 / 64.0 + s2 * k1 / 8192.0)
    c, s = np.cos(th) / 8.0, np.sin(th) / 8.0
    MA = np.zeros((2, 64, 128, 2, 64), np.float64)
    MA[0, :, :, 0, :] = c
    MA[1, :, :, 0, :] = -s
    MA[0, :, :, 1, :] = s
    MA[1, :, :, 1, :] = c
    if gathered_layout:
        pi = np.arange(64)
        s1_of_pi = ((pi // 4) % 2) * 32 + (pi // 8) * 4 + pi % 4
        MA = MA[:, s1_of_pi]
    MA = MA.reshape(128, 128 * 128).astype(ml_dtypes.bfloat16)
    CB = []
    for p in range(2):
        k2 = (64 * p + np.arange(64))[None, :]
        a2 = 2 * np.pi * np.arange(128)[:, None] * k2 / 128.0
        cb = np.stack([np.cos(a2), -np.sin(a2)], 1) / (8.0 * np.sqrt(2.0))
        CB.append(cb.reshape(128, 128).astype(ml_dtypes.bfloat16))
    ident = np.eye(128, dtype=np.float32)
    return cs, MA, CB, ident


def _bias_tab(rpb_l, p, i, t0, nkt, pad):
    kp = np.arange(128)[:, None, None]
    tt = np.arange(nkt)[None, :, None]
    q = np.arange(128)[None, None, :]
    krow = 64 * p + 2 * (t0 + tt) + kp // 64 - 4
    kc = kp % 64
    qrow = 64 * p + 2 * i + q // 64
    qc = q % 64
    r0 = np.clip(qrow - 4, 0, 120)
    c0 = np.clip(qc - 8, 0, 48)
    valid = (krow >= 0) & (krow <= 127) & (krow >= r0) & (krow <= r0 + 7) & (kc >= c0) & (kc <= c0 + 15)
    dr = np.clip(krow - qrow + 7, 0, 14)
    dc = np.clip(kc - qc, -15, 15) + 15
    dr, dc, valid = np.broadcast_arrays(dr, dc, valid)
    out = np.full((128, 8, pad, 128), NEG, np.float32)
    for h in range(8):
        out[:, h, :nkt, :] = np.where(valid, rpb_l[h][dr, dc], np.float32(NEG))
    return out


def host_bias(rpb_l, p):
    bint = _bias_tab(rpb_l, p, 2, 2, 5, 5).reshape(128, 8 * 5 * 128)
    sp = [_bias_tab(rpb_l, p, 0, 0, 6, 6), _bias_tab(rpb_l, p, 1, 1, 5, 6),
          _bias_tab(rpb_l, p, NT - 2, NT - 2, 5, 6), _bias_tab(rpb_l, p, NT - 1, NT - 2, 6, 6)]
    bsp = np.stack([t.reshape(128, 8 * 6 * 128) for t in sp], 0)
    return bint, bsp


def host_halo(kT_pair, va_pair, p):
    kTh = np.zeros((512, 72 * 64), kT_pair.dtype)
    vh = np.zeros((72 * 64, 520), va_pair.dtype)
    g0 = 64 * p - 4
    lo, hi = max(g0, 0), min(g0 + 72, 128)
    kTh[:, (lo - g0) * 64:(hi - g0) * 64] = kT_pair[:, lo * 64:hi * 64]
    vh[(lo - g0) * 64:(hi - g0) * 64, :] = va_pair[lo * 64:hi * 64, :]
    return kTh, vh


def _dram(nc, name, shape, dtype, kind):
    return nc.dram_tensor(name, list(shape), dtype, kind=kind).ap()


def _decl_F(nc, sfx):
    I = lambda n, s: _dram(nc, n + sfx, s, F32, "ExternalInput")
    return dict(wg=I("wg", [D, DFF]), wu=I("wu", [D, DFF]), wd=I("wd", [DFF, D]), lng=I("lng", [1, D]), lnb=I("lnb", [1, D]))


def _run_F(nc, S, ps, pfx, x_in, x_out, xT_out, w, ident):
    with ExitStack() as st:
        build_F(nc, S, st, pfx, x_in, x_out, xT_out, w["wg"], w["wu"], w["wd"], w["lng"][0, :], w["lnb"][0, :], ident, ps)
        S.emit()


def build_launch_A(with_prev_F):
    nc = bass.Bass("TRN2", target_bir_lowering=False)
    x = _dram(nc, "x", [TOK, D], F32, "ExternalInput")
    ident = _dram(nc, "ident", [128, 128], F32, "ExternalInput")
    cs = _dram(nc, "cs", [128, 256], BF16, "ExternalInput")
    win = _dram(nc, "win", [D, 2048], F32, "ExternalInput")
    wprev = _decl_F(nc, "_p") if with_prev_F else None
    w1 = _decl_F(nc, "_1")
    x1 = _dram(nc, "x1", [TOK, D], F32, "ExternalOutput")
    qT = _dram(nc, "qT", [512, TOK], BF16, "ExternalOutput")
    kT = _dram(nc, "kT", [512, TOK], BF16, "ExternalOutput")
    vaug = _dram(nc, "vaug", [TOK, 520], BF16, "ExternalOutput")
    ab = _dram(nc, "ab", [TOK, 1024], BF16, "ExternalOutput")
    x1T = _dram(nc, "x1T", [D, TOK], BF16, "Internal")
    x3 = _dram(nc, "x3", [TOK, D], F32, "Internal") if with_prev_F else None
    with ExitStack() as stack:
        ps = stack.enter_context(nc.psum_tensor("ps", [128, 8, 512], F32))
        S = Sched(nc, stack)
        xin = x
        if with_prev_F:
            _run_F(nc, S, ps, "fp_", x, x3, None, wprev, ident)
            xin = x3
        _run_F(nc, S, ps, "f1_", xin, x1, x1T, w1, ident)
        with ExitStack() as st:
            build_P(nc, S, st, "p_", x1T, win, cs, qT, kT, vaug, ab, ps)
            S.emit()
    return nc


def build_launch_B():
    nc = bass.Bass("TRN2", target_bir_lowering=False)
    I = lambda n, s, d=F32: _dram(nc, n, s, d, "ExternalInput")
    x1 = I("x1", [TOK, D])
    qT = I("qT", [512, TOK], BF16)
    kTh = I("kTh", [512, 72 * 64], BF16)
    vh = I("vh", [72 * 64, 520], BF16)
    ab_all = I("ab_all", [8192, 1024], BF16)
    MA = I("MA", [128, 128 * 128], BF16)
    CB = I("CB", [128, 128], BF16)
    bint = I("bint", [128, 5120])
    bsp = I("bsp", [4, 128, 6144])
    wout = I("wout", [D, D])
    ga = I("ga", [512])
    gf = I("gf", [512])
    lng = I("lng", [1, D])
    lnb = I("lnb", [1, D])
    ident = I("ident", [128, 128])
    x2 = _dram(nc, "x2", [TOK, D], F32, "ExternalOutput")
    T_d = _dram(nc, "T_d", [128, 128, 512], BF16, "Internal")
    F_d = _dram(nc, "F_d", [4, 128, TOK], BF16, "Internal")
    with ExitStack() as stack:
        ps = stack.enter_context(nc.psum_tensor("ps", [128, 8, 512], F32))
        rinvf = stack.enter_context(nc.sbuf_tensor("rinvf", [128, NT], F32))
        S = Sched(nc, stack)
        with ExitStack() as st:
            build_MA(nc, S, st, "ma_", ab_all, MA, T_d, ps)
            S.emit()
        with ExitStack() as st:
            build_MB(nc, S, st, "mb_", T_d, CB, F_d, rinvf, ps)
            S.emit()
        with ExitStack() as st:
            build_MC(nc, S, st, "mc_", x1, x2, qT, kTh, vh, F_d, rinvf, bint, bsp, wout, ga, gf, lng[0, :], lnb[0, :], ident, ps)
            S.emit()
    return nc


def build_launch_D():
    nc = bass.Bass("TRN2", target_bir_lowering=False)
    x = _dram(nc, "x", [TOK, D], F32, "ExternalInput")
    ident = _dram(nc, "ident", [128, 128], F32, "ExternalInput")
    w = _decl_F(nc, "_1")
    out = _dram(nc, "out", [TOK, D], F32, "ExternalOutput")
    with ExitStack() as stack:
        ps = stack.enter_context(nc.psum_tensor("ps", [128, 8, 512], F32))
        S = Sched(nc, stack)
        _run_F(nc, S, ps, "f1_", x, out, None, w, ident)
    return nc


def _fw(inputs, which, l, sfx):
    c = np.ascontiguousarray
    return {"wg" + sfx: c(inputs[which + "_w_gate"][l]), "wu" + sfx: c(inputs[which + "_w_up"][l]),
            "wd" + sfx: c(inputs[which + "_w_down"][l])}


PAIRS = [[0, 1], [2, 3], [4, 5], [6, 7]]


def _exchange_block(S, B, b, nblk, kT_d, vaug_d, ab_d, xch):
    aball, pack, packg = xch["aball"], xch["pack"], xch["packg"]
    S.cc(lambda e: e.collective_compute("AllGather", ALU.bypass, replica_groups=PAIRS, ins=[ab_d[b * 512:(b + 1) * 512, :]],
                                        outs=[aball[b * 1024:(b + 1) * 1024, :]]),
         reads=[B("abd%d" % b)], writes=[B("aball")])
    if b == 0:
        S.dma("pool", lambda e: e.dma_start(out=pack[:, 0:256], in_=kT_d[:, 0:256]), None, reads=[B("kTd0")], writes=[B("pack")])
        S.dma("pool", lambda e: e.dma_start(out=pack[0:256, 512:1032], in_=vaug_d[0:256, :]), None, reads=[B("vd0")], writes=[B("pack")])
    if b == nblk - 1:
        S.dma("pool", lambda e: e.dma_start(out=pack[:, 256:512], in_=kT_d[:, TOK - 256:TOK]), None,
              reads=[B("kTd%d" % b)], writes=[B("pack")])
        S.dma("pool", lambda e: e.dma_start(out=pack[256:512, 512:1032], in_=vaug_d[TOK - 256:TOK, :]), None,
              reads=[B("vd%d" % b)], writes=[B("pack")])
        S.cc(lambda e: e.collective_compute("AllGather", ALU.bypass, replica_groups=PAIRS, ins=[pack], outs=[packg]),
             reads=[B("pack")], writes=[B("packg")])


def build_fused():
    nc = bass.Bass("TRN2", target_bir_lowering=False)
    I = lambda n, s, d=F32: _dram(nc, n, s, d, "ExternalInput")
    N = lambda n, s, d=F32: _dram(nc, n, s, d, "Internal")
    x = I("x", [TOK, D])
    out = _dram(nc, "out", [TOK, D], F32, "ExternalOutput")
    ident = I("ident", [128, 128])
    cs = I("cs", [128, 256], BF16)
    MA = I("MA", [128, 128 * 128], BF16)
    CB = I("CB", [128, 128], BF16)
    L = []
    for l in range(DEPTH):
        s = "_%d" % l
        L.append(dict(
            f1=_decl_F(nc, "_a%d" % l), f2=_decl_F(nc, "_b%d" % l),
            win=I("win" + s, [D, 2048]), wout=I("wout" + s, [D, D]), ga=I("ga" + s, [512]), gf=I("gf" + s, [512]),
            lng2=I("lng2" + s, [1, D]), lnb2=I("lnb2" + s, [1, D]), bint=I("bint" + s, [128, 5120]), bsp=I("bsp" + s, [4, 128, 6144]),
            x1=N("x1" + s, [TOK, D]), x1T=N("x1T" + s, [D, TOK], BF16), qT=N("qT" + s, [512, TOK], BF16),
            kT=N("kT" + s, [512, TOK], BF16), vaug=N("vaug" + s, [TOK, 520], BF16), ab=N("ab" + s, [TOK, 1024], BF16),
            aball=N("aball" + s, [2 * TOK, 1024], BF16), pack=N("pack" + s, [512, 1032], BF16), packg=N("packg" + s, [1024, 1032], BF16),
            T=N("T" + s, [128, 128, 512], BF16), F=N("F" + s, [4, 128, TOK], BF16),
            x2=N("x2" + s, [TOK, D]), x3=(N("x3" + s, [TOK, D]) if l + 1 < DEPTH else out)))
    with ExitStack() as stack:
        ps = stack.enter_context(nc.psum_tensor("ps", [128, 8, 512], F32))
        rinvf = stack.enter_context(nc.sbuf_tensor("rinvf", [128, NT], F32))
        S = Sched(nc, stack)
        xin = x
        for l in range(DEPTH):
            t = L[l]
            _run_F(nc, S, ps, "f1_%d_" % l, xin, t["x1"], t["x1T"], t["f1"], ident)
            with ExitStack() as so:
                MA_sb = so.enter_context(nc.sbuf_tensor("MA_sb%d" % l, [128, 128, 128], BF16))
                with ExitStack() as st:
                    S.dma("sp", lambda e: e.dma_start(out=MA_sb[:].rearrange("p s m -> p (s m)"), in_=MA), None,
                          writes=[S.buf("MA_pre")], out_dram=True)
                    build_P(nc, S, st, "p%d_" % l, t["x1T"], t["win"], cs, t["qT"], t["kT"], t["vaug"], t["ab"], ps, xch=t)
                    S.emit()
                with ExitStack() as st:
                    build_MA(nc, S, st, "ma%d_" % l, t["aball"], MA, t["T"], ps, MA_pre=MA_sb)
                    S.emit()
            with ExitStack() as so:
                qT_sb = so.enter_context(nc.sbuf_tensor("qT_sb%d" % l, [128, 4, TOK], BF16))
                kT_sb = so.enter_context(nc.sbuf_tensor("kT_sb%d" % l, [128, 4, 72 * 64], BF16))
                kg, vg = t["packg"][:, 0:512], t["packg"][:, 512:1032]
                with ExitStack() as st:
                    load_qk(S, qT_sb, kT_sb, t["qT"], t["kT"], kg, out_dram=True)
                    build_MB(nc, S, st, "mb%d_" % l, t["T"], CB, t["F"], rinvf, ps)
                    S.emit()
                with ExitStack() as st:
                    build_MC(nc, S, st, "mc%d_" % l, t["x1"], t["x2"], t["qT"], t["kT"], t["vaug"], t["F"], rinvf, t["bint"], t["bsp"],
                             t["wout"], t["ga"], t["gf"], t["lng2"][0, :], t["lnb2"][0, :], ident, ps, kg_d=kg, vg_d=vg,
                             qk_pre=(qT_sb, kT_sb))
                    S.emit()
            _run_F(nc, S, ps, "f2_%d_" % l, t["x2"], t["x3"], None, t["f2"], ident)
            xin = t["x3"]
    return nc


def kernel(**inputs):
    inputs = {k: np.asarray(v) for k, v in inputs.items()}
    cs, MAc, CBc, ident = host_consts()
    c = np.ascontiguousarray
    cores = list(range(NCORES))
    xs = inputs["x"].reshape(NCORES, TOK, D)
    shared = {"ident": ident, "cs": cs, "MA": MAc}
    for l in range(DEPTH):
        s = "_%d" % l
        for nm, which, sfx in (("ffn1", "a", "ln1"), ("ffn2", "b", "ln3")):
            sf = "_%s%d" % (which, l)
            shared["wg" + sf] = c(inputs[nm + "_w_gate"][l])
            shared["wu" + sf] = c(inputs[nm + "_w_up"][l])
            shared["wd" + sf] = c(inputs[nm + "_w_down"][l])
            shared["lng" + sf] = c(inputs[sfx + "_g"][l:l + 1])
            shared["lnb" + sf] = c(inputs[sfx + "_b"][l:l + 1])
        shared["win" + s] = c(inputs["w_in"][l])
        shared["wout" + s] = c(inputs["w_out"][l])
        shared["ga" + s] = c(inputs["g_attn"][l])
        shared["gf" + s] = c(inputs["g_fourier"][l])
        shared["lng2" + s] = c(inputs["ln2_g"][l:l + 1])
        shared["lnb2" + s] = c(inputs["ln2_b"][l:l + 1])
    ims = []
    for i in cores:
        p = i % 2
        m = dict(shared, x=c(xs[i]), CB=CBc[p])
        for l in range(DEPTH):
            bint, bsp = host_bias(inputs["rpb"][l], p)
            m["bint_%d" % l] = bint
            m["bsp_%d" % l] = bsp
        ims.append(m)
    nc = build_fused()
    res = run_bass_kernel_spmd(nc, ims, core_ids=cores).results
    out = np.stack([np.asarray(res[i]["out"]) for i in cores], 0).reshape(4, 8192, D)
    return out.astype(np.float32)
```
